# Optimizing a Trainium2 kernel written in Bass

```python
import jax, jax.numpy as jnp
from jax import lax
import numpy as np

D_MODEL = 1024
BATCH = 16
SEQ = 4096
DEPTH = 1

PLE_DIM = 256
HEAD_DIM = 64
HALF_DIM = HEAD_DIM // 2
ROPE_THETA = 10000.0
BLOCK_Q = 128
LN_EPS = 1e-5
NEG_INF = -1e30
FORCE = 1e30
TINY = 1e-30

SWA_HEADS = 8
SWA_KV_HEADS = 1
SWA_WINDOW = 128

NSA_HEADS = 8
NSA_GROUPS = 2
NSA_WINDOW = 512
CMP_BLOCK = 32
CMP_STRIDE = 16
CMP_HIDDEN = 256
SEL_BLOCK = 64
N_SEL = 16
SEL_CHUNK = 64
N_NSA_BRANCH = 3

N_BRANCH = 2
D_FF = -(-8 * D_MODEL // (3 * 256)) * 256

SWA_Q = SWA_HEADS * HEAD_DIM
SWA_KV = SWA_KV_HEADS * HEAD_DIM
NSA_Q = NSA_HEADS * HEAD_DIM
NSA_KV = NSA_GROUPS * HEAD_DIM
NSA_GATES = NSA_HEADS * N_NSA_BRANCH
IN_SPLITS = (SWA_Q, SWA_KV, SWA_KV, NSA_Q, NSA_KV, NSA_KV, NSA_KV, NSA_KV, NSA_KV, NSA_KV, NSA_GATES, N_BRANCH * D_MODEL)
IN_IS_VALUE = (False, False, True, False, False, True, False, True, False, True, False, False)
D_IN = SWA_Q + 2 * SWA_KV + NSA_Q + 6 * NSA_KV + NSA_GATES + N_BRANCH * D_MODEL

kernel_name = 'hybrid_swa_sink_nsa_deepnorm_block'


def layer_norm(x, g, b):
    xf = x.astype(jnp.float32)
    mu = jnp.mean(xf, axis=-1, keepdims=True)
    var = jnp.mean(jnp.square(xf - mu), axis=-1, keepdims=True)
    return ((xf - mu) * lax.rsqrt(var + LN_EPS) * g + b).astype(x.dtype)


def rope_tables(positions, dtype):
    inv = ROPE_THETA ** (-jnp.arange(0, HEAD_DIM, 2, dtype=jnp.float32) / HEAD_DIM)
    ang = positions.astype(jnp.float32)[..., None] * inv
    return jnp.cos(ang)[:, :, None, :].astype(dtype), jnp.sin(ang)[:, :, None, :].astype(dtype)


def apply_rope(x, cos, sin):
    x1, x2 = x[..., :HALF_DIM], x[..., HALF_DIM:]
    return jnp.concatenate([x1 * cos - x2 * sin, x2 * cos + x1 * sin], axis=-1)


def masked_softmax(s, mask, sink=None):
    s = jnp.where(mask, s, NEG_INF)
    m = jnp.max(s, axis=-1, keepdims=True)
    if sink is not None:
        m = jnp.maximum(m, sink)
    e = jnp.where(mask, jnp.exp(s - m), 0.0)
    den = jnp.sum(e, axis=-1, keepdims=True)
    if sink is not None:
        den = den + jnp.exp(sink - m)
    return e / jnp.maximum(den, TINY)


def split_in(z):
    outs = []
    off = 0
    for n in IN_SPLITS:
        outs.append(z[..., off:off + n])
        off += n
    return outs


def banded_attention(q, k, v, window, sink=None):
    B, S, G, R, dh = q.shape
    n_prev = -(-window // BLOCK_Q)
    pad = n_prev * BLOCK_Q
    band = pad + BLOCK_Q
    kp = jnp.pad(k, ((0, 0), (pad, 0), (0, 0), (0, 0)))
    vp = jnp.pad(v, ((0, 0), (pad, 0), (0, 0), (0, 0)))
    scale = dh ** -0.5

    def one_block(c):
        start = c * BLOCK_Q
        qb = lax.dynamic_slice_in_dim(q, start, BLOCK_Q, axis=1)
        kb = lax.dynamic_slice_in_dim(kp, start, band, axis=1)
        vb = lax.dynamic_slice_in_dim(vp, start, band, axis=1)
        s = jnp.einsum('bqgrd,bkgd->bgrqk', qb, kb, preferred_element_type=jnp.float32) * scale
        qpos = start + jnp.arange(BLOCK_Q)
        kpos = start - pad + jnp.arange(band)
        diff = qpos[:, None] - kpos[None, :]
        mask = (diff >= 0) & (diff < window) & (kpos[None, :] >= 0)
        pr = masked_softmax(s, mask, sink)
        return jnp.einsum('bgrqk,bkgd->bqgrd', pr.astype(v.dtype), vb)

    out = lax.map(one_block, jnp.arange(S // BLOCK_Q))
    return out.transpose(1, 0, 2, 3, 4, 5).reshape(B, S, G, R, dh)


def compress_tokens(kv, pos_emb, w1, w2):
    B, S, G, dh = kv.shape
    n_sub = CMP_BLOCK // CMP_STRIDE
    chunks = kv.reshape(B, S // CMP_STRIDE, CMP_STRIDE, G, dh)
    n_cmp = S // CMP_STRIDE - n_sub + 1
    blocks = jnp.concatenate([chunks[:, j:j + n_cmp] for j in range(n_sub)], axis=2)
    blocks = blocks + pos_emb[:, None, :]
    flat = blocks.transpose(0, 1, 3, 2, 4).reshape(B, n_cmp, G, CMP_BLOCK * dh)
    return jax.nn.gelu(flat @ w1) @ w2


def selected_attention(q, k, v, sel_idx):
    B, S, G, R, dh = q.shape
    n_blk = S // SEL_BLOCK
    n_sel = sel_idx.shape[-1]
    L = n_sel * SEL_BLOCK
    kb = k.reshape(B, n_blk, SEL_BLOCK, G, dh).transpose(0, 3, 1, 2, 4)
    vb = v.reshape(B, n_blk, SEL_BLOCK, G, dh).transpose(0, 3, 1, 2, 4)
    bi = jnp.arange(B)[:, None, None, None]
    gi = jnp.arange(G)[None, :, None, None]
    scale = dh ** -0.5

    def one_chunk(c):
        start = c * SEL_CHUNK
        qc = lax.dynamic_slice_in_dim(q, start, SEL_CHUNK, axis=1)
        ic = lax.dynamic_slice_in_dim(sel_idx, start, SEL_CHUNK, axis=2)
        kg = kb[bi, gi, ic].reshape(B, G, SEL_CHUNK, L, dh)
        vg = vb[bi, gi, ic].reshape(B, G, SEL_CHUNK, L, dh)
        s = jnp.einsum('bqgrd,bgqld->bgrql', qc, kg, preferred_element_type=jnp.float32) * scale
        kpos = (ic[..., None] * SEL_BLOCK + jnp.arange(SEL_BLOCK)).reshape(B, G, SEL_CHUNK, L)
        qpos = start + jnp.arange(SEL_CHUNK)
        mask = (kpos <= qpos[:, None])[:, :, None]
        pr = masked_softmax(s, mask)
        return jnp.einsum('bgrql,bgqld->bqgrd', pr.astype(v.dtype), vg)

    out = lax.map(one_chunk, jnp.arange(S // SEL_CHUNK))
    return out.transpose(1, 0, 2, 3, 4, 5).reshape(B, S, G, R, dh)


def nsa_mixer(qn, kc, vc, ks, vs, kw, vw, gn, cos, sin, pos_emb, w_k1, w_k2, w_v1, w_v2):
    B, S, _ = qn.shape
    G, R, dh = NSA_GROUPS, NSA_HEADS // NSA_GROUPS, HEAD_DIM
    q = qn.reshape(B, S, NSA_HEADS, dh)
    q_nope = q.reshape(B, S, G, R, dh)
    q_rope = apply_rope(q, cos, sin).reshape(B, S, G, R, dh)
    t = jnp.arange(S)
    k_c = compress_tokens(kc.reshape(B, S, G, dh), pos_emb[0], w_k1, w_k2)
    v_c = compress_tokens(vc.reshape(B, S, G, dh), pos_emb[1], w_v1, w_v2)
    n_cmp = k_c.shape[1]
    s_c = jnp.einsum('bqgrd,bcgd->bgrqc', q_nope, k_c, preferred_element_type=jnp.float32) * dh ** -0.5
    cmp_end = jnp.arange(n_cmp) * CMP_STRIDE + CMP_BLOCK - 1
    p_c = masked_softmax(s_c, cmp_end[None, :] <= t[:, None])
    o_cmp = jnp.einsum('bgrqc,bcgd->bqgrd', p_c.astype(v_c.dtype), v_c)
    n_blk = S // SEL_BLOCK
    c_start = jnp.arange(n_cmp) * CMP_STRIDE
    b_start = jnp.arange(n_blk) * SEL_BLOCK
    overlap = ((c_start[:, None] < b_start[None, :] + SEL_BLOCK) & (c_start[:, None] + CMP_BLOCK > b_start[None, :])).astype(jnp.float32)
    imp = jnp.einsum('bgrqc,cj->bgqj', p_c, overlap)
    blk = jnp.arange(n_blk)[None, :]
    cur = (t // SEL_BLOCK)[:, None]
    forced = (blk == 0) | (blk == cur) | (blk == cur - 1)
    imp = jnp.where(forced, FORCE, jnp.where(blk <= cur, imp, NEG_INF))
    n_sel = min(N_SEL, n_blk)
    _, sel_idx = lax.top_k(imp, n_sel)
    k_s = apply_rope(ks.reshape(B, S, G, dh), cos, sin)
    o_slc = selected_attention(q_rope, k_s, vs.reshape(B, S, G, dh), sel_idx)
    k_w = apply_rope(kw.reshape(B, S, G, dh), cos, sin)
    o_win = banded_attention(q_rope, k_w, vw.reshape(B, S, G, dh), NSA_WINDOW)
    g = jax.nn.sigmoid(gn).reshape(B, S, G, R, N_NSA_BRANCH)
    o = g[..., 0:1] * o_cmp + g[..., 1:2] * o_slc + g[..., 2:3] * o_win
    return o.reshape(B, S, NSA_Q)


def temporal_mix(h, cos, sin, w_in, sinks, pos_emb, w_k1, w_k2, w_v1, w_v2, w_proj_swa, w_proj_nsa, w_out):
    B, S, _ = h.shape
    dh = HEAD_DIM
    r_a = SWA_HEADS // SWA_KV_HEADS
    z = h @ w_in
    qa, ka, va, qn, kc, vc, ks, vs, kw, vw, gn, gm = split_in(z)
    qa = apply_rope(qa.reshape(B, S, SWA_HEADS, dh), cos, sin).reshape(B, S, SWA_KV_HEADS, r_a, dh)
    ka = apply_rope(ka.reshape(B, S, SWA_KV_HEADS, dh), cos, sin)
    va = va.reshape(B, S, SWA_KV_HEADS, dh)
    sink = sinks.astype(jnp.float32).reshape(1, SWA_KV_HEADS, r_a, 1, 1)
    o_a = banded_attention(qa, ka, va, SWA_WINDOW, sink).reshape(B, S, SWA_Q)
    o_b = nsa_mixer(qn, kc, vc, ks, vs, kw, vw, gn, cos, sin, pos_emb, w_k1, w_k2, w_v1, w_v2)
    gates = jax.nn.sigmoid(gm).reshape(B, S, N_BRANCH, D_MODEL)
    y = gates[:, :, 0] * (o_a @ w_proj_swa) + gates[:, :, 1] * (o_b @ w_proj_nsa)
    return y @ w_out


def _w(k, shape, fan_in, scale=1.0):
    return jax.random.normal(k, shape, jnp.float32) * (scale * fan_in ** -0.5)


def setup_inputs(seed: int = 0) -> dict:
    key = jax.random.key(seed)
    ks = jax.random.split(key, 24)
    beta = (8.0 * DEPTH) ** -0.25
    col_scale = jnp.concatenate([jnp.full((n,), beta if isv else 1.0, jnp.float32) for n, isv in zip(IN_SPLITS, IN_IS_VALUE)])
    return {
        'x': jax.random.normal(ks[0], (BATCH, SEQ, D_MODEL), jnp.float32),
        'p': jax.random.normal(ks[1], (DEPTH, BATCH, SEQ, PLE_DIM), jnp.float32),
        'positions': jnp.broadcast_to(jnp.arange(SEQ, dtype=jnp.int32), (BATCH, SEQ)),
        'w_in': _w(ks[2], (DEPTH, D_MODEL, D_IN), D_MODEL) * col_scale,
        'attn_sinks': 0.5 * jax.random.normal(ks[3], (DEPTH, SWA_HEADS), jnp.float32),
        'cmp_pos_emb': 0.1 * jax.random.normal(ks[4], (DEPTH, 2, CMP_BLOCK, HEAD_DIM), jnp.float32),
        'w_cmp_k1': _w(ks[5], (DEPTH, CMP_BLOCK * HEAD_DIM, CMP_HIDDEN), CMP_BLOCK * HEAD_DIM),
        'w_cmp_k2': _w(ks[6], (DEPTH, CMP_HIDDEN, HEAD_DIM), CMP_HIDDEN),
        'w_cmp_v1': _w(ks[7], (DEPTH, CMP_BLOCK * HEAD_DIM, CMP_HIDDEN), CMP_BLOCK * HEAD_DIM),
        'w_cmp_v2': _w(ks[8], (DEPTH, CMP_HIDDEN, HEAD_DIM), CMP_HIDDEN, beta),
        'w_proj_swa': _w(ks[9], (DEPTH, SWA_Q, D_MODEL), SWA_Q, beta),
        'w_proj_nsa': _w(ks[10], (DEPTH, NSA_Q, D_MODEL), NSA_Q, beta),
        'w_out': _w(ks[11], (DEPTH, D_MODEL, D_MODEL), D_MODEL, beta),
        'ln1_g': 1.0 + 0.02 * jax.random.normal(ks[12], (DEPTH, D_MODEL), jnp.float32),
        'ln1_b': 0.02 * jax.random.normal(ks[13], (DEPTH, D_MODEL), jnp.float32),
        'w_ff_gate': _w(ks[14], (DEPTH, D_MODEL, D_FF), D_MODEL, beta),
        'w_ff_up': _w(ks[15], (DEPTH, D_MODEL, D_FF), D_MODEL, beta),
        'w_ff_down': _w(ks[16], (DEPTH, D_FF, D_MODEL), D_FF, beta),
        'w_ple': _w(ks[17], (DEPTH, PLE_DIM, D_MODEL), PLE_DIM, beta),
        'w_ple_gate': _w(ks[18], (DEPTH, D_MODEL, D_MODEL), D_MODEL),
        'ln2_g': 1.0 + 0.02 * jax.random.normal(ks[19], (DEPTH, D_MODEL), jnp.float32),
        'ln2_b': 0.02 * jax.random.normal(ks[20], (DEPTH, D_MODEL), jnp.float32),
    }


def reference(x, p, positions, w_in, attn_sinks, cmp_pos_emb, w_cmp_k1, w_cmp_k2, w_cmp_v1, w_cmp_v2, w_proj_swa, w_proj_nsa, w_out, ln1_g, ln1_b, w_ff_gate, w_ff_up, w_ff_down, w_ple, w_ple_gate, ln2_g, ln2_b):
    alpha = (2.0 * DEPTH) ** 0.25
    cos, sin = rope_tables(positions, x.dtype)
    h = x
    for i in range(DEPTH):
        mix = temporal_mix(h, cos, sin, w_in[i], attn_sinks[i], cmp_pos_emb[i], w_cmp_k1[i], w_cmp_k2[i], w_cmp_v1[i], w_cmp_v2[i], w_proj_swa[i], w_proj_nsa[i], w_out[i])
        h = layer_norm(alpha * h + mix, ln1_g[i], ln1_b[i])
        ff = (jax.nn.silu(h @ w_ff_gate[i]) * (h @ w_ff_up[i])) @ w_ff_down[i]
        ple = jax.nn.sigmoid(h @ w_ple_gate[i]) * (p[i] @ w_ple[i])
        h = layer_norm(alpha * h + ff + ple, ln2_g[i], ln2_b[i])
    return h
```

```python
import math
from contextlib import ExitStack

import numpy as np
import ml_dtypes
import concourse.bass as bass
import concourse.mybir as mybir
from concourse.bass_utils import run_bass_kernel_spmd

F32 = mybir.dt.float32
BF16 = mybir.dt.bfloat16
I32 = mybir.dt.int32
AF = mybir.ActivationFunctionType
ALU = mybir.AluOpType

S = 4096
DM = 1024
NSEQ = 2
DFF = 2816
NEGB = -30000.0
ALPHA = 2.0 ** 0.25
NWA = 3416
DBG = {}
DEBUG_OUT = ("QT_A", "QT_R", "QT_N", "KT_A", "KT_S", "KT_W", "KT_C", "VT_C", "VTM", "EGN", "KCT_D", "VCA_D", "OT_A", "OT_B", "wab")


class Buf:
    __slots__ = ("name", "w", "r", "sem", "excl")

    def __init__(self, name, excl=False):
        self.name = name
        self.w = {}
        self.r = {}
        self.sem = None
        self.excl = excl


class Trk:
    def __init__(self, nc, stack):
        self.nc = nc
        self.stack = stack
        self.eng = {"pe": nc.tensor, "act": nc.scalar, "dve": nc.vector, "pool": nc.gpsimd, "sp": nc.sync}
        self.sems = {}
        self.cnt = {}
        self.seen = {e: {} for e in self.eng}
        self.pending = {e: ([], []) for e in self.eng}
        for e in self.eng:
            self._sem("E_" + e)
        self.n_inst = 0
        self.n_wait = 0

    def _sem(self, key):
        if key not in self.sems:
            self.sems[key] = self.stack.enter_context(self.nc.semaphore(key))
            self.cnt[key] = 0
        return self.sems[key]

    def _wait(self, e, conds):
        seen = self.seen[e]
        need = {}
        for (k, v) in conds:
            if seen.get(k, 0) >= v:
                continue
            if need.get(k, 0) < v:
                need[k] = v
        for k, v in need.items():
            self.eng[e].wait_ge(self.sems[k], v)
            seen[k] = v
            self.n_wait += 1

    def op(self, e, fn, reads=(), writes=(), inc=True):
        conds = []
        own = "E_" + e
        writes = list(writes) + [b for b in reads if b.excl]
        reads = [b for b in reads if not b.excl]
        for b in reads:
            conds += list(b.w.items())
        for b in writes:
            conds += list(b.w.items())
            conds += list(b.r.items())
        if e == "pe":
            conds = [c for c in conds if c[0] != own]
        self._wait(e, conds)
        ins = fn()
        self.n_inst += 1
        pr, pw = self.pending[e]
        pr += list(reads)
        pw += list(writes)
        if inc:
            self.cnt[own] += 1
            v = self.cnt[own]
            ins.then_inc(self.sems[own], 1)
            for b in pr:
                b.r[own] = v
            for b in pw:
                b.w = {own: v}
                b.r = {}
            self.pending[e] = ([], [])
        return ins

    def dma(self, q, out, in_, reads, writes, sembuf, **kw):
        conds = []
        for b in reads:
            conds += list(b.w.items())
        for b in writes:
            conds += list(b.w.items())
            conds += list(b.r.items())
        if sembuf.sem is None:
            sembuf.sem = "D_" + sembuf.name
        self._sem(sembuf.sem)
        k = sembuf.sem
        conds = [c for c in conds if c[0] != k]
        self._wait(q, conds)
        self.cnt[k] += 16
        v = self.cnt[k]
        ins = self.eng[q].dma_start(out=out, in_=in_, **kw)
        ins.then_inc(self.sems[k], 16)
        self.n_inst += 1
        for b in reads:
            b.r[k] = v
        for b in writes:
            b.w = {k: v}
            b.r = {}
        return ins

    def barrier(self):
        allc = [(k, v) for k, v in self.cnt.items() if v > 0 and not k.startswith("D_prep")]
        for e in self.eng:
            self._wait(e, allc)


class Rot:
    def __init__(self, items):
        self.items = items
        self.i = 0

    def next(self):
        it = self.items[self.i % len(self.items)]
        self.i += 1
        return it


def _perm_half(cols):
    cols = np.asarray(cols).reshape(-1, 64)
    return np.concatenate([cols[:, 32:], cols[:, :32]], axis=1).reshape(-1)


def _wa_columns():
    qa0, ka0, va0, qn0, kc0, vc0, ks0, vs0, kw0, vw0, gn0 = 0, 512, 576, 640, 1152, 1280, 1408, 1536, 1664, 1792, 1920
    r = np.arange
    chunks = []
    qa_pairs = [np.concatenate([qa0 + j * 64 + r(64), qa0 + (4 + j) * 64 + r(64)]) for j in range(4)]
    qn_pairs = [np.concatenate([qn0 + j * 64 + r(64), qn0 + (4 + j) * 64 + r(64)]) for j in range(4)]
    chunks += qa_pairs
    chunks += [_perm_half(c) for c in qa_pairs]
    kad = np.concatenate([ka0 + r(64), ka0 + r(64)])
    chunks += [kad, _perm_half(kad)]
    chunks += qn_pairs
    chunks += [_perm_half(c) for c in qn_pairs]
    ks = ks0 + r(128)
    kw = kw0 + r(128)
    chunks += [ks, _perm_half(ks), kw, _perm_half(kw)]
    chunks += [kc0 + r(128), vc0 + r(128)]
    tm = np.concatenate([va0 + r(64), vs0 + r(128), vw0 + r(128), gn0 + r(24)])
    cols = np.concatenate(chunks + [tm])
    assert cols.shape[0] == NWA
    return cols


def _consts():
    bf = ml_dtypes.bfloat16
    c = {}
    c["ident"] = np.eye(128, dtype=np.float32).astype(bf)
    c["identf"] = np.eye(128, dtype=np.float32)
    k = np.arange(128)[:, None]
    q = np.arange(128)[None, :]
    diag = np.where(k <= q, 0.0, NEGB).astype(np.float32)
    upper = np.where(k > q, 0.0, NEGB).astype(np.float32)
    c["bdiag4"] = np.tile(diag, (1, 4)).astype(bf)
    c["bupper4"] = np.tile(upper, (1, 4)).astype(bf)
    rr = np.arange(128)[:, None]
    tbl = np.where(16 * (rr - 120) + 31 <= q, 0.0, NEGB).astype(np.float32)
    c["tbl4"] = np.tile(tbl, (1, 4)).astype(bf)
    m = np.arange(256)[None, :]
    c["wide"] = (m == rr + 8).astype(np.float32).astype(bf)
    j = np.arange(64)[:, None]
    mm = np.arange(4096)[None, :]
    eall = np.zeros((128, 4096), np.float32)
    eall[:64] = (mm // 64 == j)
    eall[64:] = eall[:64]
    c["eall"] = eall.astype(bf)
    d = np.arange(128)[None, :] - 62
    s_ = (np.arange(128)[:, None] >= 64).astype(np.int64)
    adj = np.zeros((128, 128), np.float32)
    adj = np.where(d > s_, -1e30, adj)
    adj = np.where((d == s_) | (d == s_ - 1), 1e30, adj)
    c["adjtbl"] = adj.astype(np.float32)
    cs = (np.arange(256) * 16)[:, None]
    bs = (np.arange(64) * 64)[None, :]
    ov = ((cs < bs + 64) & (cs + 32 > bs)).astype(np.float32)
    ov[255] = 0.0
    c["ov"] = ov.reshape(2, 128, 64).transpose(1, 0, 2).copy().astype(bf)
    p = np.arange(128)
    inv = (10000.0 ** (-(2.0 * (p % 32)) / 64.0)).astype(np.float32)
    sign = np.where((p % 64) < 32, -1.0, 1.0).astype(np.float32)
    c["ropec"] = np.stack([inv, sign], axis=1).astype(np.float32)
    return c


def build_nc(debug=False, nseq=NSEQ, phases="ABCD"):
    nc = bass.Bass("TRN2", target_bir_lowering=False)
    st = ExitStack()
    T = Trk(nc, st)

    def din(name, shape, dt):
        return nc.dram_tensor(name, list(shape), dt, kind="ExternalInput").ap()

    def dscr(name, shape, dt, out=False):
        return nc.dram_tensor(name, list(shape), dt, kind=("ExternalOutput" if (out or (debug and name.rstrip("01") in DEBUG_OUT)) else "Internal")).ap()

    xT_in = din("xT", [NSEQ, DM, S], F32)
    x_in = din("x", [NSEQ, S, DM], F32)
    pT_in = din("pT", [NSEQ, 256, S], F32)
    pos_in = din("pos", [NSEQ, S], I32)
    wa_in = din("wa", [DM, NWA], F32)
    wgm_in = din("wgm", [DM, 2048], F32)
    sinks_in = din("sinks", [1, 8], F32)
    peT_in = din("peT", [128, 2, 16], F32)
    w1k_in = din("w1k", [128, 16, 256], F32)
    w1v_in = din("w1v", [128, 16, 256], F32)
    w2k_in = din("w2k", [256, 64], F32)
    w2v_in = din("w2v", [256, 64], F32)
    wpa_in = din("wpa", [512, DM], F32)
    wpb_in = din("wpb", [512, DM], F32)
    wout_in = din("wout", [DM, DM], F32)
    wg_in = din("wg", [DM, DFF], F32)
    wu_in = din("wu", [DM, DFF], F32)
    wd_in = din("wd", [DFF, DM], F32)
    wple_in = din("wple", [256, DM], F32)
    wpg_in = din("wpg", [DM, DM], F32)
    ln_in = din("ln", [4, DM], F32)
    c_ident = din("c_ident", [128, 128], BF16)
    c_bdiag4 = din("c_bdiag4", [128, 512], BF16)
    c_bupper4 = din("c_bupper4", [128, 512], BF16)
    c_tbl4 = din("c_tbl4", [128, 512], BF16)
    c_wide = din("c_wide", [128, 256], BF16)
    c_eall = din("c_eall", [128, 4096], BF16)
    c_adjtbl = din("c_adjtbl", [128, 128], F32)
    c_ov = din("c_ov", [128, 2, 64], BF16)
    c_ropec = din("c_ropec", [128, 2], F32)
    c_identf = din("c_identf", [128, 128], F32)
    lncol_in = din("lncol", [128, 2, 8], F32)

    out_d = nc.dram_tensor("out", [NSEQ, S, DM], F32, kind="ExternalOutput").ap()

    xTb = dscr("xTb", [NSEQ, DM, S], BF16)
    pTb = dscr("pTb", [NSEQ, 256, S], BF16)
    wab = dscr("wab", [DM, NWA], BF16)
    wgmb = dscr("wgmb", [DM, 2048], BF16)
    w1kb = dscr("w1kb", [128, 16, 256], BF16)
    w1vb = dscr("w1vb", [128, 16, 256], BF16)
    w2kb = dscr("w2kb", [256, 64], BF16)
    w2vb = dscr("w2vb", [256, 64], BF16)
    peTb = dscr("peTb", [128, 2, 16], BF16)
    wpab = dscr("wpab", [512, DM], BF16)
    wpbb = dscr("wpbb", [512, DM], BF16)
    woutb = dscr("woutb", [DM, DM], BF16)
    wgb = dscr("wgb", [DM, DFF], BF16)
    wub = dscr("wub", [DM, DFF], BF16)
    wdb = dscr("wdb", [DFF, DM], BF16)
    wpleb = dscr("wpleb", [256, DM], BF16)
    wpgb = dscr("wpgb", [DM, DM], BF16)

    SCR = []
    DD = []
    for q_ in range(NSEQ):
        sc = {}
        sc["QT_A"] = dscr("QT_A%d" % q_, [128, 4, S], BF16)
        sc["QT_R"] = dscr("QT_R%d" % q_, [128, 4, S], BF16)
        sc["QT_N"] = dscr("QT_N%d" % q_, [128, 4, S], BF16)
        sc["KT_A"] = dscr("KT_A%d" % q_, [128, S], BF16)
        sc["KT_S"] = dscr("KT_S%d" % q_, [128, S], BF16)
        sc["KT_W"] = dscr("KT_W%d" % q_, [128, S], BF16)
        sc["KT_C"] = dscr("KT_C%d" % q_, [128, S], BF16)
        sc["VT_C"] = dscr("VT_C%d" % q_, [128, S], BF16)
        sc["VTM"] = dscr("VTM%d" % q_, [S, 320], BF16)
        sc["EGN"] = dscr("EGN%d" % q_, [S, 24], F32)
        sc["KCT_D"] = dscr("KCT_D%d" % q_, [128, 256], BF16)
        sc["VCA_D"] = dscr("VCA_D%d" % q_, [128, 2, 2, 64], BF16)
        sc["OT_A"] = dscr("OT_A%d" % q_, [128, 4, S], BF16)
        sc["OT_B"] = dscr("OT_B%d" % q_, [128, 4, S], BF16)
        SCR.append(sc)
        DD.append({n: Buf("d_%s%d" % (n, q_)) for n in list(sc.keys()) + ["out"]})

    def scr(q_):
        sc = SCR[q_]
        return [sc[n] for n in ['QT_A', 'QT_R', 'QT_N', 'KT_A', 'KT_S', 'KT_W', 'KT_C', 'VT_C', 'VTM', 'EGN', 'KCT_D', 'VCA_D', 'OT_A', 'OT_B']] + [DD[q_]]

    B_prepA = Buf("prepA")
    B_prepM = Buf("prepM")
    B_prepC = Buf("prepC")
    B_prepX = [Buf("prepX%d" % i) for i in range(NSEQ)]
    B_prepP = [Buf("prepP%d" % i) for i in range(NSEQ)]

    PS = [st.enter_context(nc.psum_tensor("ps%d" % i, [128, 512], F32)) for i in range(8)]
    PB = [Buf("ps%d" % i, excl=True) for i in range(8)]

    uid = [0]

    def sbuf(stack, name, shape, dt):
        uid[0] += 1
        return stack.enter_context(nc.sbuf_tensor("%s_u%d" % (name, uid[0]), list(shape), dt))

    pace = Buf("pace")
    prep_items = []

    def cast2d(dst, src, rows, buf, paced, split=128):
        for r0 in range(0, rows, split):
            r1 = min(rows, r0 + split)
            if paced:
                prep_items.append(lambda d_=dst[r0:r1], s_=src[r0:r1], b_=buf: T.dma("pool", d_, s_, [pace], [b_], b_))
            else:
                T.dma("pool", dst[r0:r1], src[r0:r1], [], [buf], buf)

    def prep_small():
        for (d_, s_, rows) in [(w1kb, w1k_in, 128), (w1vb, w1v_in, 128), (w2kb, w2k_in, 256), (w2vb, w2v_in, 256),
                               (peTb, peT_in, 128)]:
            cast2d(d_, s_, rows, B_prepM, False)

    def prep_big():
        for (d_, s_, rows) in [(wgmb, wgm_in, DM), (wpab, wpa_in, 512), (wpbb, wpb_in, 512), (woutb, wout_in, DM)]:
            cast2d(d_, s_, rows, B_prepC, True)
        cast2d(xTb[0], xT_in[0], DM, B_prepX[0], True)
        for (d_, s_, rows) in [(wgb, wg_in, DM), (wub, wu_in, DM), (wdb, wd_in, DFF), (wpleb, wple_in, 256), (wpgb, wpg_in, DM)]:
            cast2d(d_, s_, rows, B_prepC, True)
        cast2d(pTb[0], pT_in[0], 256, B_prepP[0], True)
        for s in range(1, nseq):
            cast2d(xTb[s], xT_in[s], DM, B_prepX[s], True)
            cast2d(pTb[s], pT_in[s], 256, B_prepP[s], True)

    def prep_emit(n, pace_src=None):
        if pace_src is not None:
            pace.w = dict(pace_src.w)
        for _ in range(n):
            if prep_items:
                prep_items.pop(0)()

    def phase_a(seqs):
        with ExitStack() as sk:
            WA = sbuf(sk, "A_WA", [128, 8, NWA], BF16); bWAk = [Buf("A_WA%d" % k) for k in range(8)]
            xTs = [sbuf(sk, "A_xT%d" % i, [128, 8, 512], BF16) for i in range(2)]
            bxT = [Buf("A_xT%d" % i) for i in range(2)]
            posi = [sbuf(sk, "A_pos%d" % i, [128, 512], I32) for i in range(2)]
            bpos = [Buf("A_pos%d" % i) for i in range(2)]
            ropec = sbuf(sk, "A_ropec", [128, 2], F32); bropec = Buf("A_ropec")
            halfpi = sbuf(sk, "A_halfpi", [128, 1], F32); bhalfpi = Buf("A_halfpi")
            ang = sbuf(sk, "A_ang", [128, 512], F32); bang = Buf("A_ang")
            tu = sbuf(sk, "A_tu", [128, 512], F32); btu = Buf("A_tu")
            tki = sbuf(sk, "A_tki", [128, 512], I32); btki = Buf("A_tki")
            tkf = sbuf(sk, "A_tkf", [128, 512], F32); btkf = Buf("A_tkf")
            tr = sbuf(sk, "A_tr", [128, 512], F32); btr = Buf("A_tr")
            cosT = sbuf(sk, "A_cos", [128, 512], F32); bcos = Buf("A_cos")
            sinT = sbuf(sk, "A_sin", [128, 512], F32); bsin = Buf("A_sin")
            t1 = [sbuf(sk, "A_t1_%d" % i, [128, 512], F32) for i in range(2)]
            bt1 = [Buf("A_t1_%d" % i) for i in range(2)]
            t2 = [sbuf(sk, "A_t2_%d" % i, [128, 512], F32) for i in range(2)]
            bt2 = [Buf("A_t2_%d" % i) for i in range(2)]
            stg = [sbuf(sk, "A_stg%d" % i, [128, 512], BF16) for i in range(4)]
            bstg = [Buf("A_stg%d" % i) for i in range(4)]
            vst = [sbuf(sk, "A_vst%d" % i, [128, 320], BF16) for i in range(2)]
            bvst = [Buf("A_vst%d" % i) for i in range(2)]
            est = [sbuf(sk, "A_est%d" % i, [128, 24], F32) for i in range(2)]
            best = [Buf("A_est%d" % i) for i in range(2)]
            rs = Rot(list(range(4)))
            rt = Rot(list(range(2)))
            rv = Rot(list(range(2)))
            rp = Rot(list(range(8)))

            T.dma("sp", ropec[:], c_ropec, [], [bropec], bropec)
            T.op("dve", lambda: nc.vector.memset(halfpi[:], math.pi / 2), [], [bhalfpi])
            wav = wa_in.rearrange("(k p) n -> p k n", p=128)

            def load_x(mt):
                i = mt % 2
                for k in range(8):
                    if mt == 0 and s == seqs[0]:
                        T.dma("pool", WA[:, k, :], wav[:, k, :], [], [bWAk[k]], bWAk[k])
                    T.dma("pool", xTs[i][:, k, :], xv[:, k, mt * 512:(mt + 1) * 512], [], [bxT[i]], bxT[i])
                T.dma("sp", posi[i][:], pos_in[s:s + 1, mt * 512:(mt + 1) * 512].to_broadcast([128, 512]), [], [bpos[i]], bpos[i])

            C1 = 6.28125
            C2 = 2.0 * math.pi - 6.28125

            def trig(dst, bdst, shift, use_sign):
                T.op("dve", lambda: nc.vector.tensor_scalar(out=tu[:], in0=ang[:], scalar1=1.0 / (2 * math.pi),
                                                           scalar2=shift / (2 * math.pi), op0=ALU.mult, op1=ALU.add),
                     [bang], [btu])
                T.op("dve", lambda: nc.vector.tensor_copy(out=tki[:], in_=tu[:]), [btu], [btki])
                T.op("dve", lambda: nc.vector.tensor_copy(out=tkf[:], in_=tki[:]), [btki], [btkf])
                T.op("dve", lambda: nc.vector.scalar_tensor_tensor(out=tr[:], in0=tkf[:], scalar=-C1, in1=ang[:],
                                                                  op0=ALU.mult, op1=ALU.add), [btkf, bang], [btr])
                T.op("dve", lambda: nc.vector.scalar_tensor_tensor(out=tu[:], in0=tkf[:], scalar=-C2, in1=tr[:],
                                                                  op0=ALU.mult, op1=ALU.add), [btkf, btr], [btu])
                lim = math.pi - shift - 1e-5
                T.op("dve", lambda: nc.vector.tensor_scalar(out=tu[:], in0=tu[:], scalar1=-math.pi - shift + 1e-5,
                                                           scalar2=lim, op0=ALU.max, op1=ALU.min), [btu], [btu])
                if shift != 0.0:
                    T.op("act", lambda: nc.scalar.activation(out=dst[:], in_=tu[:], func=AF.Sin, bias=halfpi[:, 0:1]),
                         [btu, bhalfpi], [bdst])
                else:
                    T.op("act", lambda: nc.scalar.activation(out=tr[:], in_=tu[:], func=AF.Sin), [btu], [btr])
                    T.op("dve", lambda: nc.vector.tensor_scalar(out=dst[:], in0=tr[:], scalar1=ropec[:, 1:2], scalar2=None,
                                                               op0=ALU.mult), [btr, bropec], [bdst])

            for s in seqs:
                (QT_A, QT_R, QT_N, KT_A, KT_S, KT_W, KT_C, VT_C, VTM, EGN, KCT_D, VCA_D, OT_A, OT_B, D) = scr(s)
                xv = xT_in[s].rearrange("(k p) n -> p k n", p=128)
                load_x(0)
                for mt in range(8 if DBG.get("a_mt") is None else DBG["a_mt"]):
                    i = mt % 2
                    if mt + 1 < 8:
                        load_x(mt + 1)
                    tok = slice(mt * 512, (mt + 1) * 512)
                    T.op("dve", lambda: nc.vector.tensor_copy(out=tkf[:], in_=posi[i][:]), [bpos[i]], [btkf])
                    T.op("dve", lambda: nc.vector.tensor_scalar(out=ang[:], in0=tkf[:], scalar1=ropec[:, 0:1], scalar2=None,
                                                               op0=ALU.mult), [btkf, bropec], [bang])
                    trig(sinT, bsin, 0.0, True)
                    trig(cosT, bcos, math.pi / 2, False)
                    if DBG.get("a_stop") == 1:
                        continue

                    def fm(chunk, pb):
                        for k in range(8):
                            T.op("pe", lambda: nc.tensor.matmul(PS[pb][:], lhsT=WA[:, k, chunk * 128:(chunk + 1) * 128],
                                                               rhs=xTs[i][:, k, :], start=(k == 0), stop=(k == 7)),
                                 [bWAk[k], bxT[i]], [PB[pb]], inc=(k == 7))

                    def roped(cz, cp, dst_ap, dbuf, extra=None):
                        pz = rp.next(); pp = rp.next()
                        fm(cz, pz); fm(cp, pp)
                        a = rt.next()
                        T.op("dve", lambda: nc.vector.tensor_tensor(out=t1[a][:], in0=PS[pz][:], in1=cosT[:], op=ALU.mult),
                             [PB[pz], bcos], [bt1[a]])
                        T.op("dve", lambda: nc.vector.tensor_tensor(out=t2[a][:], in0=PS[pp][:], in1=sinT[:], op=ALU.mult),
                             [PB[pp], bsin], [bt2[a]])
                        g = rs.next()
                        T.op("dve", lambda: nc.vector.tensor_tensor(out=stg[g][:], in0=t1[a][:], in1=t2[a][:], op=ALU.add),
                             [bt1[a], bt2[a]], [bstg[g]])
                        T.dma("sp", dst_ap, stg[g][:], [bstg[g]], [dbuf], bstg[g])
                        if extra is not None:
                            g2 = rs.next()
                            T.op("act", lambda: nc.scalar.copy(out=stg[g2][:], in_=PS[pz][:]), [PB[pz]], [bstg[g2]])
                            T.dma("sp", extra[0], stg[g2][:], [bstg[g2]], [extra[1]], bstg[g2])

                    def plain(cz, dst_ap, dbuf):
                        pz = rp.next()
                        fm(cz, pz)
                        g = rs.next()
                        T.op("act", lambda: nc.scalar.copy(out=stg[g][:], in_=PS[pz][:]), [PB[pz]], [bstg[g]])
                        T.dma("sp", dst_ap, stg[g][:], [bstg[g]], [dbuf], bstg[g])

                    for j in range(4):
                        roped(j, 4 + j, QT_A[:, j, tok], D["QT_A"])
                        if DBG.get("a_stop") == 2:
                            break
                    if DBG.get("a_stop") == 2:
                        continue
                    roped(8, 9, KT_A[:, tok], D["KT_A"])
                    if DBG.get("a_stop") == 4:
                        continue
                    for j in range(4):
                        roped(10 + j, 14 + j, QT_R[:, j, tok], D["QT_R"], extra=(QT_N[:, j, tok], D["QT_N"]))
                    if DBG.get("a_stop") == 5:
                        continue
                    roped(18, 19, KT_S[:, tok], D["KT_S"])
                    roped(20, 21, KT_W[:, tok], D["KT_W"])
                    plain(22, KT_C[:, tok], D["KT_C"])
                    plain(23, VT_C[:, tok], D["VT_C"])
                    if DBG.get("a_stop") == 3:
                        continue
                    for t in range(4):
                        pb = rp.next()
                        for k in range(8):
                            T.op("pe", lambda: nc.tensor.matmul(PS[pb][:, 0:344], lhsT=xTs[i][:, k, t * 128:(t + 1) * 128],
                                                               rhs=WA[:, k, 3072:3416], start=(k == 0), stop=(k == 7)),
                                 [bWAk[k], bxT[i]], [PB[pb]], inc=(k == 7))
                        v = rv.next()
                        T.op("dve", lambda: nc.vector.tensor_copy(out=vst[v][:], in_=PS[pb][:, 0:320]), [PB[pb]], [bvst[v]])
                        T.op("act", lambda: nc.scalar.activation(out=est[v][:], in_=PS[pb][:, 320:344], func=AF.Exp, scale=-1.0),
                             [PB[pb]], [best[v]])
                        r0 = mt * 512 + t * 128
                        T.dma("sp", VTM[r0:r0 + 128, :], vst[v][:], [bvst[v]], [D["VTM"]], bvst[v])
                        T.dma("sp", EGN[r0:r0 + 128, :], est[v][:], [best[v]], [D["EGN"]], best[v])
        T.barrier()

    def phase_cmp(s):
        (QT_A, QT_R, QT_N, KT_A, KT_S, KT_W, KT_C, VT_C, VTM, EGN, KCT_D, VCA_D, OT_A, OT_B, D) = scr(s)
        with ExitStack() as sk:
            XC = [[sbuf(sk, "M_XC%d_%d" % (i, g), [128, S], BF16) for g in range(2)] for i in range(2)]
            bXC = [Buf("M_XC%d" % i) for i in range(2)]
            W1 = [sbuf(sk, "M_W1_%d" % i, [128, 16, 256], BF16) for i in range(2)]
            bW1 = [Buf("M_W1_%d" % i) for i in range(2)]
            W2 = [sbuf(sk, "M_W2_%d" % i, [128, 2, 64], BF16) for i in range(2)]
            bW2 = Buf("M_W2")
            peT = sbuf(sk, "M_peT", [128, 2, 16], BF16); bpeT = Buf("M_peT")
            bias = sbuf(sk, "M_bias", [128, 4], F32); bbias = Buf("M_bias")
            H = sbuf(sk, "M_H", [128, 2, 2, 2, 256], BF16)
            bH = [[Buf("M_H%d%d" % (a, b)) for b in range(2)] for a in range(2)]
            xs = sbuf(sk, "M_xs", [128, 256], F32); bxs = Buf("M_xs")
            xu = sbuf(sk, "M_xu", [128, 256], F32); bxu = Buf("M_xu")
            xg = sbuf(sk, "M_xg", [128, 256], F32); bxg = Buf("M_xg")
            kct = sbuf(sk, "M_kct", [128, 256], BF16); bkct = Buf("M_kct")
            vca = sbuf(sk, "M_vca", [128, 2, 2, 64], BF16); bvca = Buf("M_vca")
            rp = Rot(list(range(8)))

            for kv, (src, dn_) in enumerate([(KT_C, "KT_C"), (VT_C, "VT_C")]):
                for g in range(2):
                    gs_ = slice(g * 64, (g + 1) * 64)
                    T.op("pool", lambda: nc.gpsimd.memset(XC[kv][g][64:128, S - 16:S], 0.0), [], [bXC[kv]])
                    T.dma("sp", XC[kv][g][0:64, :], src[gs_, :], [D[dn_]], [bXC[kv]], bXC[kv])
                    T.dma("sp", XC[kv][g][64:128, 0:S - 16], src[gs_, 16:S], [D[dn_]], [bXC[kv]], bXC[kv])
            T.dma("sp", W1[0][:], w1kb, [B_prepM], [bW1[0]], bW1[0])
            T.dma("sp", W1[1][:], w1vb, [B_prepM], [bW1[1]], bW1[1])
            T.dma("sp", W2[0][:], w2kb.rearrange("(k p) n -> p k n", p=128), [B_prepM], [bW2], bW2)
            T.dma("sp", W2[1][:], w2vb.rearrange("(k p) n -> p k n", p=128), [B_prepM], [bW2], bW2)
            T.dma("sp", peT[:], peTb, [B_prepM], [bpeT], bpeT)
            T.op("dve", lambda: nc.vector.memset(kct[:], 0.0), [], [bkct])
            T.op("dve", lambda: nc.vector.memset(vca[:], 0.0), [], [bvca])
            T.op("dve", lambda: nc.vector.memset(H[:], 0.0), [], [bH[0][0], bH[0][1], bH[1][0], bH[1][1]])
            pbb = rp.next()
            for kv in range(2):
                for hc in range(2):
                    col = kv * 2 + hc
                    for j in range(16):
                        T.op("pe", lambda: nc.tensor.matmul(PS[pbb][:, col:col + 1], lhsT=W1[kv][:, j, hc * 128:(hc + 1) * 128],
                                                           rhs=peT[:, kv, j:j + 1], start=(j == 0), stop=(j == 15)),
                             [bW1[kv], bpeT], [PB[pbb]], inc=(j == 15))
            T.op("dve", lambda: nc.vector.tensor_copy(out=bias[:], in_=PS[pbb][:, 0:4]), [PB[pbb]], [bbias])
            for kv in range(2):
                for g in range(2):
                    xcv = XC[kv][g][:].rearrange("p (c s) -> p c s", s=16)
                    for hc in range(2):
                        pb = rp.next()
                        for j in range(16):
                            T.op("pe", lambda: nc.tensor.matmul(
                                PS[pb][:, 0:255], lhsT=W1[kv][:, j, hc * 128:(hc + 1) * 128],
                                rhs=xcv[:, 0:255, j],
                                start=(j == 0), stop=(j == 15)), [bW1[kv], bXC[kv]], [PB[pb]], inc=(j == 15))
                        col = kv * 2 + hc
                        T.op("act", lambda: nc.scalar.activation(out=xs[:, 0:255], in_=PS[pb][:, 0:255], func=AF.Identity,
                                                                bias=bias[:, col:col + 1]), [PB[pb], bbias], [bxs])
                        T.op("dve", lambda: nc.vector.tensor_tensor(out=xu[:, 0:255], in0=xs[:, 0:255], in1=xs[:, 0:255], op=ALU.mult),
                             [bxs], [bxu])
                        T.op("dve", lambda: nc.vector.tensor_scalar(out=xu[:, 0:255], in0=xu[:, 0:255], scalar1=0.044715, scalar2=1.0,
                                                                   op0=ALU.mult, op1=ALU.add), [bxu], [bxu])
                        T.op("dve", lambda: nc.vector.tensor_tensor(out=xu[:, 0:255], in0=xu[:, 0:255], in1=xs[:, 0:255], op=ALU.mult),
                             [bxu, bxs], [bxu])
                        T.op("act", lambda: nc.scalar.activation(out=xg[:, 0:255], in_=xu[:, 0:255], func=AF.Sigmoid,
                                                                scale=2.0 * math.sqrt(2.0 / math.pi)), [bxu], [bxg])
                        T.op("dve", lambda: nc.vector.tensor_tensor(out=H[:, kv, g, hc, 0:255], in0=xg[:, 0:255], in1=xs[:, 0:255],
                                                                   op=ALU.mult), [bxg, bxs], [bH[kv][g]])
            pk = rp.next()
            for g in range(2):
                for hc in range(2):
                    T.op("pe", lambda: nc.tensor.matmul(PS[pk][g * 64:(g + 1) * 64, 0:256], lhsT=W2[0][:, hc, :], rhs=H[:, 0, g, hc, :],
                                                       start=(hc == 0), stop=(hc == 1)), [bW2, bH[0][g]], [PB[pk]], inc=(hc == 1))
            T.op("dve", lambda: nc.vector.tensor_copy(out=kct[:, 0:255], in_=PS[pk][:, 0:255]), [PB[pk]], [bkct])
            T.dma("sp", KCT_D, kct[:], [bkct], [D["KCT_D"]], bkct)
            pv = rp.next()
            for cc in range(2):
                rows = 128 if cc == 0 else 127
                for g in range(2):
                    for hc in range(2):
                        T.op("pe", lambda: nc.tensor.matmul(PS[pv][0:rows, (cc * 2 + g) * 64:(cc * 2 + g + 1) * 64],
                                                           lhsT=H[:, 1, g, hc, cc * 128:cc * 128 + rows], rhs=W2[1][:, hc, :],
                                                           start=(hc == 0), stop=(hc == 1)), [bW2, bH[1][g]], [PB[pv]], inc=(hc == 1))
            T.op("dve", lambda: nc.vector.tensor_copy(out=vca[:, 0, :, :], in_=PS[pv][:, 0:128].rearrange("p (g d) -> p g d", g=2)),
                 [PB[pv]], [bvca])
            T.op("dve", lambda: nc.vector.tensor_copy(out=vca[0:127, 1, :, :], in_=PS[pv][0:127, 128:256].rearrange("p (g d) -> p g d", g=2)),
                 [PB[pv]], [bvca])
            T.dma("sp", VCA_D, vca[:], [bvca], [D["VCA_D"]], bvca)

    def phase_b(s, qts=None):
        (QT_A, QT_R, QT_N, KT_A, KT_S, KT_W, KT_C, VT_C, VTM, EGN, KCT_D, VCA_D, OT_A, OT_B, D) = scr(s)
        with ExitStack() as sk:
            KST = sbuf(sk, "B_KST", [128, S], BF16); bKST = Buf("B_KST")
            VS = sbuf(sk, "B_VS", [128, 32, 2, 65], BF16); bVS = Buf("B_VS")
            KcT = sbuf(sk, "B_KcT", [128, 256], BF16); bKcT = Buf("B_KcT")
            VcA = sbuf(sk, "B_VcA", [128, 2, 2, 65], BF16); bVcA = Buf("B_VcA")
            ident = sbuf(sk, "B_ident", [128, 128], BF16)
            bdiag4 = sbuf(sk, "B_bdiag4", [128, 512], BF16)
            bupper4 = sbuf(sk, "B_bupper4", [128, 512], BF16)
            tbl4 = sbuf(sk, "B_tbl4", [128, 512], BF16)
            wide = sbuf(sk, "B_wide", [128, 256], BF16)
            KSE = [sbuf(sk, "B_KSE%d" % g, [128, S], BF16) for g in range(2)]
            bKSE = Buf("B_KSE")
            adjtbl = sbuf(sk, "B_adjtbl", [128, 128], F32)
            ov = sbuf(sk, "B_ov", [128, 2, 64], BF16)
            bC = Buf("B_const")
            esink = sbuf(sk, "B_esink", [128, 8], F32); besink = Buf("B_esink")
            NR = 3
            QAZ = [[sbuf(sk, "B_QAZ%d_%d" % (i, h), [128, 4, 128], BF16) for h in range(2)] for i in range(NR)]
            QNZ = [[sbuf(sk, "B_QNZ%d_%d" % (i, h), [128, 4, 128], BF16) for h in range(2)] for i in range(NR)]
            QZ = [[sbuf(sk, "B_QZ%d_%d" % (i, h), [128, 4, 128], BF16) for h in range(2)] for i in range(NR)]
            QB = [[sbuf(sk, "B_QB%d_%d" % (i, h), [128, 4, 128], BF16) for h in range(2)] for i in range(NR)]
            bQB = [[Buf("B_QB%d_%d" % (i, h)) for h in range(2)] for i in range(NR)]
            KAw = [sbuf(sk, "B_KAw%d" % i, [128, 256], BF16) for i in range(NR)]
            KWw = [sbuf(sk, "B_KWw%d" % i, [128, 640], BF16) for i in range(NR)]
            VAw = [sbuf(sk, "B_VAw%d" % i, [128, 2, 65], BF16) for i in range(NR)]
            VWw = [sbuf(sk, "B_VWw%d" % i, [128, 5, 2, 65], BF16) for i in range(NR)]
            Eg = [sbuf(sk, "B_Eg%d" % i, [128, 24], F32) for i in range(NR)]
            bIn = [Buf("B_in%d" % i) for i in range(NR)]
            NP = 4
            Pt = [sbuf(sk, "B_P%d" % i, [128, 512], BF16) for i in range(NP)]
            bP = [Buf("B_P%d" % i) for i in range(NP)]
            rP = Rot(list(range(NP)))
            oa2 = [sbuf(sk, "B_oa%d" % i, [128, 512], BF16) for i in range(2)]; boa2 = [Buf("B_oa%d" % i) for i in range(2)]
            ob2 = [sbuf(sk, "B_ob%d" % i, [128, 512], BF16) for i in range(2)]; bob2 = [Buf("B_ob%d" % i) for i in range(2)]
            oTs = [sbuf(sk, "B_oTs%d" % i, [128, 4, 128], BF16) for i in range(4)]
            boTs = [Buf("B_oTs%d" % i) for i in range(4)]
            roT = Rot(list(range(4)))
            dn = sbuf(sk, "B_dn", [128, 8], F32); bdn = Buf("B_dn")
            den3 = sbuf(sk, "B_den3", [128, 4, 3], F32); bden3 = Buf("B_den3")
            coef = sbuf(sk, "B_coef", [128, 4, 3], F32); bcoef = Buf("B_coef")
            rdc = sbuf(sk, "B_rdc", [128, 4], F32); brdc = Buf("B_rdc")
            imp = sbuf(sk, "B_imp", [128, 64], F32); bimp = Buf("B_imp")
            iw = sbuf(sk, "B_iw", [128, 64], F32); biw = Buf("B_iw")
            m8 = sbuf(sk, "B_m8", [128, 8], F32); bm8 = Buf("B_m8")
            bm = sbuf(sk, "B_bm", [128, 64], BF16); bbm = Buf("B_bm")
            accS = sbuf(sk, "B_accS", [128, 3, 260], F32); bacS = Buf("B_accS")
            egS = sbuf(sk, "B_egS", [128, 12], F32)
            tmp4 = sbuf(sk, "B_tmp4", [128, 4, 64], F32); btmp = Buf("B_tmp4")
            tmp4b = sbuf(sk, "B_tmp4b", [128, 4, 64], F32); btmpb = Buf("B_tmp4b")

            S0, S1, TB, A_SWA, A_CMP, A_IMP, A_SLC, A_WIN = range(8)
            rS = Rot([S0, S1, TB])
            PTV = {b_: PS[b_][:].bitcast(BF16) for b_ in (S0, S1, TB)}

            for a in range(4):
                T.dma("sp", KST[:, a * 1024:(a + 1) * 1024], KT_S[:, a * 1024:(a + 1) * 1024], [D["KT_S"]], [bKST], bKST)
            T.op("dve", lambda: nc.vector.memset(VS[:], 1.0), [], [bVS])
            for g in range(2):
                vsv = VTM[:, 64 + g * 64:128 + g * 64].rearrange("(c p) d -> p c d", p=128)
                for a in range(4):
                    T.dma("sp", VS[:, a * 8:(a + 1) * 8, g, 0:64], vsv[:, a * 8:(a + 1) * 8, :], [D["VTM"]], [bVS], bVS)
            phase_cmp(s)
            T.dma("sp", KcT[:], KCT_D, [D["KCT_D"]], [bKcT], bKcT)
            T.op("dve", lambda: nc.vector.memset(VcA[:], 1.0), [], [bVcA])
            for cc in range(2):
                T.dma("sp", VcA[:, cc, :, 0:64], VCA_D[:, cc], [D["VCA_D"]], [bVcA], bVcA)
            for (t_, c_) in [(ident, c_ident), (bdiag4, c_bdiag4), (bupper4, c_bupper4), (tbl4, c_tbl4), (wide, c_wide),
                             (adjtbl, c_adjtbl), (ov, c_ov)]:
                T.dma("sp", t_[:], c_, [], [bC], bC)
            T.dma("sp", esink[:], sinks_in.to_broadcast([128, 8]), [], [besink], besink)
            T.op("act", lambda: nc.scalar.activation(out=esink[:], in_=esink[:], func=AF.Exp), [besink], [besink])
            for g in range(2):
                gs_ = slice(g * 64, (g + 1) * 64)
                ot_ = slice((1 - g) * 64, (2 - g) * 64)
                for a in range(4):
                    T.dma("sp", KSE[g][gs_, a * 1024:(a + 1) * 1024], KT_S[gs_, a * 1024:(a + 1) * 1024], [D["KT_S"]], [bKSE], bKSE)
                    T.dma("sp", KSE[g][ot_, a * 1024:(a + 1) * 1024], c_eall[ot_, a * 1024:(a + 1) * 1024], [], [bKSE], bKSE)
            for i in range(NR):
                for h in range(2):
                    for tl in (QAZ, QNZ, QZ):
                        T.op("pool", lambda: nc.gpsimd.memset(tl[i][h][:], 0.0), [], [bIn[i]])
                    T.op("pool", lambda: nc.gpsimd.memset(QB[i][h][:], 0.0), [], [bIn[i], bQB[i][h]])
                T.op("dve", lambda: nc.vector.memset(VAw[i][:], 1.0), [], [bIn[i]])
                T.op("dve", lambda: nc.vector.memset(VWw[i][:], 1.0), [], [bIn[i]])

            qlist = list(range(32)) if qts is None else qts

            def load_q(qt):
                i = qt % NR
                b = bIn[i]
                q0 = qt * 128
                for h in range(2):
                    hs_ = slice(h * 64, (h + 1) * 64)
                    T.dma("sp", QAZ[i][h][hs_], QT_A[hs_, :, q0:q0 + 128], [D["QT_A"]], [b], b)
                    T.dma("sp", QNZ[i][h][hs_], QT_N[hs_, :, q0:q0 + 128], [D["QT_N"]], [b], b)
                    T.dma("sp", QZ[i][h][hs_], QT_R[hs_, :, q0:q0 + 128], [D["QT_R"]], [b], b)
                    if qt >= 8:
                        T.dma("sp", QB[i][h][hs_], QT_R[hs_, :, q0:q0 + 128], [D["QT_R"]], [b], b)
                lo = max(0, qt - 1)
                n = qt + 1 - lo
                T.dma("sp", KAw[i][:, (2 - n) * 128:256], KT_A[:, lo * 128:(qt + 1) * 128], [D["KT_A"]], [b], b)
                T.dma("sp", VAw[i][:, 2 - n:2, 0:64],
                      VTM[lo * 128:(qt + 1) * 128, 0:64].rearrange("(c p) d -> p c d", p=128), [D["VTM"]], [b], b)
                lo = max(0, qt - 4)
                n = qt + 1 - lo
                T.dma("sp", KWw[i][:, (5 - n) * 128:640], KT_W[:, lo * 128:(qt + 1) * 128], [D["KT_W"]], [b], b)
                for g in range(2):
                    T.dma("sp", VWw[i][:, 5 - n:5, g, 0:64],
                          VTM[lo * 128:(qt + 1) * 128, 192 + g * 64:256 + g * 64].rearrange("(c p) d -> p c d", p=128), [D["VTM"]], [b], b)
                T.dma("sp", Eg[i][:], EGN[q0:q0 + 128, :], [D["EGN"]], [b], b)

            def scores(lhsT, rhs, K_bufs, extra=(), M=128):
                pb = rS.next()
                n = 1 + len(extra)
                T.op("pe", lambda: nc.tensor.matmul(PS[pb][0:M, :].rearrange("p (h q) -> p h q", h=4), lhsT=lhsT, rhs=rhs,
                                                   start=True, stop=(n == 1)), K_bufs, [PB[pb]], inc=(n == 1))
                for e_i, (l2, r2, b2) in enumerate(extra):
                    last = (e_i == len(extra) - 1)
                    o2 = PS[pb][0:M, :]
                    if len(r2.shape) == 3:
                        o2 = o2.rearrange("p (h q) -> p h q", h=4)
                    T.op("pe", lambda: nc.tensor.matmul(o2, lhsT=l2, rhs=r2, start=False, stop=last), b2, [PB[pb]], inc=last)
                p = rP.next()
                T.op("act", lambda: nc.scalar.activation(out=Pt[p][0:M, :], in_=PS[pb][0:M, :], func=AF.Exp, scale=0.125),
                     [PB[pb]], [bP[p]])
                return p

            def pv(acc, p, rhs_fn, rbufs, ncol, last, M=128):
                for h in range(4):
                    T.op("pe", lambda: nc.tensor.matmul(PS[acc][:, h * ncol:(h + 1) * ncol], lhsT=Pt[p][0:M, h * 128:(h + 1) * 128],
                                                       rhs=rhs_fn, start=False, stop=last, skip_group_check=True),
                         [bP[p]] + rbufs, [PB[acc]], inc=(h == 3))

            def emit_pv(job, p):
                M = job.get("M", 128)
                for (acc, rhs_ap, rbufs, ncol) in job["pvs"]:
                    pv(acc, p, rhs_ap, rbufs, ncol, job["last"], M=M)
                if job.get("post"):
                    job["post"]()

            LA = 2

            def run_jobs(jobs):
                pend = []
                for job in jobs:
                    if job.get("pre"):
                        job["pre"]()
                    p = scores(job["lhsT"], job["rhs"], job["kb"], extra=job.get("extra", ()), M=job.get("M", 128))
                    pend.append((job, p))
                    if len(pend) > LA:
                        emit_pv(*pend.pop(0))
                while pend:
                    emit_pv(*pend.pop(0))

            def mk_memset(banks_cols):
                def f():
                    for (bk, ncols) in banks_cols:
                        T.op("dve", lambda: nc.vector.memset(PS[bk][:, 0:ncols], 0.0), [], [PB[bk]])
                return f

            def chain(*fns):
                def f():
                    for fn in fns:
                        if fn is not None:
                            fn()
                return f

            pending_tr = [None]
            pending_math = [None]
            load_q(qlist[0])
            if len(qlist) > 1:
                load_q(qlist[1])
            for qi, qt in enumerate(qlist):
                i = qt % NR
                bi = bIn[i]
                if qi + 2 < len(qlist):
                    load_q(qlist[qi + 2])
                q0 = qt * 128
                use_sel = qt >= 8
                oa, boa, ob, bob = oa2[qi % 2], boa2[qi % 2], ob2[qi % 2], bob2[qi % 2]

                def mk_swa_post(half, acc):
                    def f():
                        accv = PS[acc][:, 0:260].rearrange("p (h c) -> p h c", c=65)
                        T.op("dve", lambda: nc.vector.tensor_tensor(out=dn[:, 0:4], in0=accv[:, :, 64], in1=esink[:, half * 4:(half + 1) * 4],
                                                                   op=ALU.add), [PB[acc], besink], [bdn])
                        T.op("dve", lambda: nc.vector.reciprocal(out=dn[:, 4:8], in_=dn[:, 0:4]), [bdn], [bdn])
                        T.op("dve", lambda: nc.vector.tensor_tensor(out=oa[:, half * 256:(half + 1) * 256].rearrange("p (h d) -> p h d", h=4),
                                                                   in0=accv[:, :, 0:64], in1=dn[:, 4:8].unsqueeze(2).to_broadcast([128, 4, 64]),
                                                                   op=ALU.mult), [PB[acc], bdn], [boa])
                    return f

                def mk_topk(g):
                    oth = slice((1 - g) * 64, (2 - g) * 64)

                    def f():
                        cv = PS[A_CMP][:, 0:260].rearrange("p (h c) -> p h c", c=65)
                        T.op("dve", lambda: nc.vector.tensor_scalar(out=rdc[:], in0=cv[:, :, 64], scalar1=1e-30, scalar2=None,
                                                                   op0=ALU.max), [PB[A_CMP]], [brdc])
                        T.op("dve", lambda: nc.vector.reciprocal(out=rdc[:], in_=rdc[:]), [brdc], [brdc])
                        T.op("dve", lambda: nc.vector.tensor_scalar(out=imp[:], in0=PS[A_IMP][:, 0:64], scalar1=rdc[:, 0:1], scalar2=None,
                                                                   op0=ALU.mult), [PB[A_IMP], brdc], [bimp])
                        for h in range(1, 4):
                            T.op("dve", lambda: nc.vector.scalar_tensor_tensor(out=imp[:], in0=PS[A_IMP][:, h * 64:(h + 1) * 64],
                                                                              scalar=rdc[:, h:h + 1], in1=imp[:], op0=ALU.mult, op1=ALU.add),
                                 [PB[A_IMP], brdc, bimp], [bimp])
                        a0 = 62 - 2 * qt
                        T.op("dve", lambda: nc.vector.tensor_tensor(out=iw[:], in0=imp[:], in1=adjtbl[:, a0:a0 + 64], op=ALU.add),
                             [bimp, bC], [biw])
                        T.op("dve", lambda: nc.vector.memset(iw[:, 0:1], 1e30), [], [biw])
                        T.op("dve", lambda: nc.vector.max(out=m8[:], in_=iw[:]), [biw], [bm8])
                        T.op("dve", lambda: nc.vector.match_replace(out=iw[:], in_to_replace=m8[:], in_values=iw[:], imm_value=-3e30),
                             [biw, bm8], [biw])
                        T.op("dve", lambda: nc.vector.max(out=m8[:], in_=iw[:]), [biw], [bm8])
                        T.op("dve", lambda: nc.vector.match_replace(out=iw[:], in_to_replace=m8[:], in_values=iw[:], imm_value=-3e30),
                             [biw, bm8], [biw])
                        T.op("dve", lambda: nc.vector.tensor_scalar(out=bm[:], in0=iw[:], scalar1=-2e30, scalar2=NEGB,
                                                                   op0=ALU.is_ge, op1=ALU.mult), [biw], [bbm])

                    def f2():
                        tb = rS.next()
                        T.op("pe", lambda: nc.tensor.transpose(PTV[tb][oth, 0:128], bm[:], ident[:]), [bbm, bC], [PB[tb]])
                        T.op("dve", lambda: nc.vector.tensor_copy(out=QB[i][g][oth, :, :],
                                                                 in_=PTV[tb][oth, 0:128].unsqueeze(1).to_broadcast([64, 4, 128])),
                             [PB[tb]], [bQB[i][g]])
                    return f, f2

                def mk_combine(g, i=i, ob=ob, bob=bob):
                    def f_copy():
                        for br, acc in enumerate([A_CMP, A_SLC, A_WIN]):
                            T.op("dve", lambda: nc.vector.tensor_copy(out=accS[:, br, :], in_=PS[acc][:, 0:260]), [PB[acc]], [bacS])
                        T.op("dve", lambda: nc.vector.tensor_copy(out=egS[:], in_=Eg[i][:, g * 12:(g + 1) * 12]), [bIn[i]], [bacS])

                    def f_math():
                        def av(br):
                            return accS[:, br, :].rearrange("p (h c) -> p h c", c=65)
                        for br in range(3):
                            T.op("dve", lambda: nc.vector.tensor_scalar(out=den3[:, :, br], in0=av(br)[:, :, 64], scalar1=1e-30, scalar2=None,
                                                                       op0=ALU.max), [bacS], [bden3])
                        egv = egS[:].rearrange("p (h b) -> p h b", b=3)
                        T.op("dve", lambda: nc.vector.scalar_tensor_tensor(out=coef[:], in0=egv, scalar=1.0, in1=den3[:],
                                                                          op0=ALU.add, op1=ALU.mult), [bacS, bden3], [bcoef])
                        T.op("dve", lambda: nc.vector.reciprocal(out=coef[:], in_=coef[:]), [bcoef], [bcoef])

                        def cb(br):
                            return coef[:, :, br:br + 1].to_broadcast([128, 4, 64])
                        T.op("dve", lambda: nc.vector.tensor_tensor(out=tmp4[:], in0=av(0)[:, :, 0:64], in1=cb(0), op=ALU.mult), [bacS, bcoef], [btmp])
                        T.op("dve", lambda: nc.vector.tensor_tensor(out=tmp4b[:], in0=av(1)[:, :, 0:64], in1=cb(1), op=ALU.mult), [bacS, bcoef], [btmpb])
                        T.op("dve", lambda: nc.vector.tensor_tensor(out=tmp4[:], in0=tmp4[:], in1=tmp4b[:], op=ALU.add), [btmp, btmpb], [btmp])
                        T.op("dve", lambda: nc.vector.tensor_tensor(out=tmp4b[:], in0=av(2)[:, :, 0:64], in1=cb(2), op=ALU.mult), [bacS, bcoef], [btmpb])
                        T.op("dve", lambda: nc.vector.tensor_tensor(out=ob[:, g * 256:(g + 1) * 256].rearrange("p (h d) -> p h d", h=4),
                                                                   in0=tmp4[:], in1=tmp4b[:], op=ALU.add), [btmp, btmpb], [bob])
                    return f_copy, f_math

                for g in range(2):
                    gs = slice(g * 64, (g + 1) * 64)
                    half = g
                    j_cmp = []
                    cl = []
                    if qt <= 16:
                        cl.append((0, min(128, 8 * (qt + 1)), True))
                    else:
                        cl.append((0, 128, False))
                    if qt >= 16:
                        cl.append((1, 8 * (qt + 1) - 128, True))
                    for ci, (cc, M, masked) in enumerate(cl):
                        extra = []
                        if masked:
                            off = 8 + 120 + cc * 128 - 8 * qt
                            extra = [(wide[:, off:off + M], tbl4[:], [bC])]
                        j_cmp.append(dict(pre=mk_memset([(A_CMP, 260), (A_IMP, 256)]) if ci == 0 else None,
                                          lhsT=KcT[:, cc * 128:cc * 128 + M], rhs=QNZ[i][g][:, :, :], kb=[bKcT, bi], extra=extra, M=M,
                                          pvs=[(A_CMP, VcA[0:M, cc, g, :], [bVcA], 65), (A_IMP, ov[0:M, cc, :], [bC], 64)],
                                          last=(ci == len(cl) - 1)))
                    j_swa = []
                    chunks = ([0] if qt >= 1 else []) + [1]
                    for ci, c in enumerate(chunks):
                        bias_t = bupper4 if c == 0 else bdiag4
                        last = ci == len(chunks) - 1
                        j_swa.append(dict(pre=mk_memset([(A_SWA, 260)]) if ci == 0 else None,
                                          lhsT=KAw[i][:, c * 128:(c + 1) * 128], rhs=QAZ[i][half][:, :, :], kb=[bi],
                                          extra=[(ident[:], bias_t[:], [bC])],
                                          pvs=[(A_SWA, VAw[i][:, c, :], [bi], 65)], last=last,
                                          post=mk_swa_post(half, A_SWA) if last else None))
                    j_win = []
                    topk_dve, topk_pe = mk_topk(g) if use_sel else (None, None)
                    lo = max(0, qt - 4)
                    for kc in range(lo, qt + 1):
                        w = kc - (qt - 4)
                        extra = []
                        if kc == qt - 4:
                            extra.append((ident[:], bupper4[:], [bC]))
                        if kc == qt:
                            extra.append((ident[:], bdiag4[:], [bC]))
                        j_win.append(dict(pre=mk_memset([(A_WIN, 260)]) if kc == lo else None,
                                          lhsT=KWw[i][:, w * 128:(w + 1) * 128], rhs=QZ[i][g][:, :, :], kb=[bi], extra=extra,
                                          pvs=[(A_WIN, VWw[i][:, w, g, :], [bi], 65)], last=(kc == qt)))
                    j_slc = []
                    for kc in range(qt + 1):
                        extra = []
                        if kc == qt:
                            extra.append((ident[:], bdiag4[:], [bC]))
                        pre = None
                        if kc == 0:
                            pre = chain(topk_pe, mk_memset([(A_SLC, 260)]))
                        if use_sel:
                            l_, r_, kb_ = KSE[g][:, kc * 128:(kc + 1) * 128], QB[i][g][:, :, :], [bKSE, bi, bQB[i][g]]
                        else:
                            l_, r_, kb_ = KST[:, kc * 128:(kc + 1) * 128], QZ[i][g][:, :, :], [bKST, bi]
                        post = None
                        if kc == qt:
                            cmb_copy, cmb_math = mk_combine(g)
                            post = cmb_copy
                        j_slc.append(dict(pre=pre, lhsT=l_, rhs=r_, kb=kb_, extra=extra,
                                          pvs=[(A_SLC, VS[:, kc, g, :], [bVS], 65)], last=(kc == qt), post=post))
                    jobs = j_cmp + j_swa + j_win + j_slc
                    if use_sel:
                        jx = len(j_cmp) - 1 + LA + 1
                        assert jx < len(j_cmp) + len(j_swa) + len(j_win)
                        jobs[jx]["pre"] = chain(jobs[jx].get("pre"), topk_dve)
                    if pending_math[0] is not None:
                        jx = len(jobs) - len(j_slc) + min(2, len(j_slc) - 1)
                        jobs[jx]["pre"] = chain(jobs[jx].get("pre"), pending_math[0])
                        pending_math[0] = None
                    if g == 1 and pending_tr[0] is not None:
                        jx = min(4, len(jobs) - 1)
                        jobs[jx]["pre"] = chain(jobs[jx].get("pre"), pending_tr[0])
                        pending_tr[0] = None
                    run_jobs(jobs)
                    pending_math[0] = cmb_math
                def mk_tr(oa, boa, ob, bob, q0):
                    def f():
                        for (src, bsrc, dst, dname) in [(oa, boa, OT_A, "OT_A"), (ob, bob, OT_B, "OT_B")]:
                            tb = rS.next()
                            for k in range(4):
                                T.op("pe", lambda: nc.tensor.transpose(PTV[tb][:, k * 128:(k + 1) * 128], src[:, k * 128:(k + 1) * 128], ident[:]),
                                     [bsrc, bC], [PB[tb]], inc=(k == 3))
                            o = roT.next()
                            T.op("act", lambda: nc.scalar.copy(out=oTs[o][:], in_=PTV[tb][:, 0:512].rearrange("p (k t) -> p k t", k=4)),
                                 [PB[tb]], [boTs[o]])
                            T.dma("sp", dst[:, :, q0:q0 + 128], oTs[o][:], [boTs[o]], [D[dname]], boTs[o])
                    return f
                pending_tr[0] = mk_tr(oa, boa, ob, bob, q0)
                prep_emit(1 if qt < 8 else (3 if qt < 16 else 4), boa)
            if pending_math[0] is not None:
                pending_math[0]()
                pending_math[0] = None
            if pending_tr[0] is not None:
                pending_tr[0]()
                pending_tr[0] = None
        T.barrier()

    def phase_c(s, mts=None):
        (QT_A, QT_R, QT_N, KT_A, KT_S, KT_W, KT_C, VT_C, VTM, EGN, KCT_D, VCA_D, OT_A, OT_B, D) = scr(s)
        with ExitStack() as sk:
            NW = 9
            WP = [sbuf(sk, "C_W%d" % i, [128, 8, 512], BF16) for i in range(NW)]
            bWP = [Buf("C_W%d" % i) for i in range(NW)]
            rW = Rot(list(range(NW)))
            xTs = sbuf(sk, "C_xT", [128, 8, 512], BF16); bxT = Buf("C_xT")
            pTs = sbuf(sk, "C_pT", [128, 2, 512], BF16); bpT = Buf("C_pT")
            oTa = sbuf(sk, "C_oTa", [128, 4, 512], BF16); boTa = Buf("C_oTa")
            oTb = sbuf(sk, "C_oTb", [128, 4, 512], BF16); boTb = Buf("C_oTb")
            xtm = sbuf(sk, "C_xtm", [128, 4, DM], F32)
            bxtm = [Buf("C_xtm%d" % t) for t in range(4)]
            lnp = sbuf(sk, "C_lnp", [128, 4, DM], F32); blnp = Buf("C_lnp")
            lncol = sbuf(sk, "C_lncol", [128, 2, 8], F32); blncol = Buf("C_lncol")
            G = [sbuf(sk, "C_G%d" % i, [128, 2, 512], BF16) for i in range(2)]
            bG = [Buf("C_G%d" % i) for i in range(2)]
            rG = Rot([0, 1])
            YT = sbuf(sk, "C_YT", [128, 8, 512], BF16)
            bYT = [Buf("C_YT%d" % f) for f in range(8)]
            h1T = sbuf(sk, "C_h1T", [128, 8, 512], BF16)
            bh1T = [Buf("C_h1T%d" % t) for t in range(4)]
            bh1T2 = [Buf("C_h1Tb%d" % t) for t in range(4)]
            actT = sbuf(sk, "C_actT", [128, 22, 512], BF16)
            bactT = [Buf("C_actT%d" % f) for f in range(22)]
            acc = sbuf(sk, "C_acc", [128, 4, DM], F32)
            bacc = [Buf("C_acc%d" % t) for t in range(4)]
            tA = [sbuf(sk, "C_tA%d" % i, [128, 512], F32) for i in range(2)]
            btA = [Buf("C_tA%d" % i) for i in range(2)]
            rA = Rot([0, 1])
            stt = sbuf(sk, "C_stt", [128, 8, 2, 6], F32)
            mv = sbuf(sk, "C_mv", [128, 8, 4], F32)
            bst = [Buf("C_st%d" % j) for j in range(8)]
            epst = sbuf(sk, "C_eps", [128, 1], F32); beps = Buf("C_eps")
            outs = [sbuf(sk, "C_out%d" % i, [128, DM], F32) for i in range(2)]
            bouts = [Buf("C_out%d" % i) for i in range(2)]
            rO = Rot([0, 1])
            identf = sbuf(sk, "C_identf", [128, 128], F32); bident = Buf("C_identf")
            rp = Rot(list(range(8)))

            T.dma("sp", identf[:], c_identf, [], [bident], bident)
            T.dma("sp", lncol[:], lncol_in, [], [blncol], blncol)
            T.op("dve", lambda: nc.vector.memset(epst[:], 1e-5), [], [beps])
            for r in range(4):
                T.dma("sp", lnp[:, r, :], ln_in[r:r + 1, :].to_broadcast([128, DM]), [], [blnp], blnp)

            w_gate = []
            for fh in range(2):
                w_gate += [(wgmb, 0, 8, fh * 512, 512), (wgmb, 0, 8, 1024 + fh * 512, 512), (wpab, 0, 4, fh * 512, 512), (wpbb, 0, 4, fh * 512, 512)]
            w_mix = [(woutb, 0, 8, n * 512, 512) for n in range(2)]
            w_rest = []
            for f0 in range(0, 22, 4):
                nfl = min(4, 22 - f0)
                w_rest += [(wgb, 0, 8, f0 * 128, nfl * 128), (wub, 0, 8, f0 * 128, nfl * 128)]
            for n in range(2):
                w_rest += [(wdb, k0, nk, n * 512, 512) for (k0, nk) in [(0, 8), (8, 8), (16, 6)]]
            w_rest += [(wpgb, 0, 8, n * 512, 512) for n in range(2)] + [(wpleb, 0, 2, n * 512, 512) for n in range(2)]
            n_mt = 8 if mts is None else len(mts)
            wseq = list(w_gate)
            for mi_ in range(n_mt):
                wseq += w_mix + (w_gate if mi_ + 1 < n_mt else []) + w_rest
            wstate = {"issued": 0, "used": 0}
            PF = 4

            def wload(src2d, k0, nk, c0, ncol):
                u = wstate["used"]
                assert wseq[u] == (src2d, k0, nk, c0, ncol), (u, wseq[u][1:], (k0, nk, c0, ncol))
                while wstate["issued"] < min(len(wseq), u + PF + 1):
                    j = wstate["issued"]
                    (sr, a0, an, b0, bn) = wseq[j]
                    v = sr.rearrange("(k p) n -> p k n", p=128)
                    T.dma("sp", WP[j % NW][:, 0:an, 0:bn], v[:, a0:a0 + an, b0:b0 + bn], [B_prepC], [bWP[j % NW]], bWP[j % NW])
                    wstate["issued"] += 1
                wstate["used"] += 1
                return u % NW

            def ln_stats(src_ap, bsrc, j):
                for hh in range(2):
                    T.op("dve", lambda: nc.vector.bn_stats(out=stt[:, j, hh, :], in_=src_ap[:, hh * 512:(hh + 1) * 512]), [bsrc], [bst[j]])
                T.op("dve", lambda: nc.vector.bn_aggr(out=mv[:, j, 0:2], in_=stt[:, j, :, :].rearrange("p a b -> p (a b)")), [bst[j]], [bst[j]])
                T.op("act", lambda: nc.scalar.activation(out=mv[:, j, 2:3], in_=mv[:, j, 1:2], func=AF.Sqrt, bias=epst[:, 0:1]),
                     [bst[j], beps], [bst[j]])
                T.op("dve", lambda: nc.vector.reciprocal(out=mv[:, j, 2:3], in_=mv[:, j, 2:3]), [bst[j]], [bst[j]])
                T.op("dve", lambda: nc.vector.scalar_tensor_tensor(out=mv[:, j, 3:4], in0=mv[:, j, 0:1], scalar=-1.0, in1=mv[:, j, 2:3],
                                                                  op0=ALU.mult, op1=ALU.mult), [bst[j]], [bst[j]])
                T.op("act", lambda: nc.scalar.activation(out=src_ap, in_=src_ap, func=AF.Identity, scale=mv[:, j, 2:3], bias=mv[:, j, 3:4]),
                     [bsrc, bst[j]], [bsrc])

            mlist = list(range(8)) if mts is None else mts
            xv = xTb[s].rearrange("(k p) n -> p k n", p=128)
            pv_ = pTb[s].rearrange("(k p) n -> p k n", p=128)
            def load_gate_inputs(mt_):
                tk = slice(mt_ * 512, (mt_ + 1) * 512)
                for k in range(8):
                    T.dma("sp", xTs[:, k, :], xv[:, k, tk], [B_prepX[s]], [bxT], bxT)
                T.dma("sp", oTa[:], OT_A[:, :, tk], [D["OT_A"]], [boTa], boTa)
                T.dma("sp", oTb[:], OT_B[:, :, tk], [D["OT_B"]], [boTb], boTb)

            load_gate_inputs(mlist[0])
            def gate_step():
                for fh in range(2):
                    wg0 = wload(wgmb, 0, 8, fh * 512, 512)
                    wg1 = wload(wgmb, 0, 8, 1024 + fh * 512, 512)
                    wpa = wload(wpab, 0, 4, fh * 512, 512)
                    wpb = wload(wpbb, 0, 4, fh * 512, 512)
                    for fl in range(4):
                        f = fh * 4 + fl
                        cs = slice(fl * 128, (fl + 1) * 128)
                        gi = rG.next()
                        pg = [rp.next(), rp.next()]
                        for a, wgx in enumerate([wg0, wg1]):
                            for k in range(8):
                                T.op("pe", lambda: nc.tensor.matmul(PS[pg[a]][:], lhsT=WP[wgx][:, k, cs], rhs=xTs[:, k, :],
                                                                   start=(k == 0), stop=(k == 7)), [bWP[wgx], bxT], [PB[pg[a]]], inc=(k == 7))
                            T.op("act", lambda: nc.scalar.activation(out=G[gi][:, a, :], in_=PS[pg[a]][:], func=AF.Sigmoid),
                                 [PB[pg[a]]], [bG[gi]])
                        pa = rp.next(); pb = rp.next()
                        for (pp, wx, ox, box) in [(pa, wpa, oTa, boTa), (pb, wpb, oTb, boTb)]:
                            for k in range(4):
                                T.op("pe", lambda: nc.tensor.matmul(PS[pp][:], lhsT=WP[wx][:, k, cs], rhs=ox[:, k, :],
                                                                   start=(k == 0), stop=(k == 3)), [bWP[wx], box], [PB[pp]], inc=(k == 3))
                        a_ = rA.next()
                        T.op("dve", lambda: nc.vector.tensor_tensor(out=tA[a_][:], in0=PS[pa][:], in1=G[gi][:, 0, :], op=ALU.mult),
                             [PB[pa], bG[gi]], [btA[a_]])
                        b_ = rA.next()
                        T.op("dve", lambda: nc.vector.tensor_tensor(out=tA[b_][:], in0=PS[pb][:], in1=G[gi][:, 1, :], op=ALU.mult),
                             [PB[pb], bG[gi]], [btA[b_]])
                        T.op("dve", lambda: nc.vector.tensor_tensor(out=YT[:, f, :], in0=tA[a_][:], in1=tA[b_][:], op=ALU.add),
                             [btA[a_], btA[b_]], [bYT[f]])

            def mix_ln1(mt):
                tok = slice(mt * 512, (mt + 1) * 512)
                T.dma("sp", pTs[:], pv_[:, :, tok], [B_prepP[s]], [bpT], bpT)
                for t in range(4):
                    r0 = mt * 512 + t * 128
                    T.dma("sp", xtm[:, t, :], x_in[s, r0:r0 + 128, :], [], [bxtm[t]], bxtm[t])
                wo = [wload(woutb, 0, 8, n * 512, 512) for n in range(2)]
                for t in range(4):
                    for n in range(2):
                        pb = rp.next()
                        for k in range(8):
                            T.op("pe", lambda: nc.tensor.matmul(PS[pb][:], lhsT=YT[:, k, t * 128:(t + 1) * 128], rhs=WP[wo[n]][:, k, :],
                                                               start=(k == 0), stop=(k == 7)), [bYT[k], bWP[wo[n]]], [PB[pb]], inc=(k == 7))
                        T.op("dve", lambda: nc.vector.scalar_tensor_tensor(out=xtm[:, t, n * 512:(n + 1) * 512], in0=xtm[:, t, n * 512:(n + 1) * 512],
                                                                          scalar=ALPHA, in1=PS[pb][:], op0=ALU.mult, op1=ALU.add),
                             [bxtm[t], PB[pb]], [bxtm[t]])
                    ln_stats(xtm[:, t, :], bxtm[t], t)
                for t in range(4):
                    for hh in range(2):
                        pb = rp.next()
                        for j in range(4):
                            k = hh * 4 + j
                            T.op("pe", lambda: nc.tensor.transpose(PS[pb][:, j * 128:(j + 1) * 128], xtm[:, t, k * 128:(k + 1) * 128], identf[:]),
                                 [bxtm[t], bident], [PB[pb]], inc=(j == 3))
                        for j in range(4):
                            k = hh * 4 + j
                            if hh == 0:
                                T.op("act", lambda: nc.scalar.activation(out=h1T[:, k, t * 128:(t + 1) * 128], in_=PS[pb][:, j * 128:(j + 1) * 128],
                                                                        func=AF.Identity, scale=lncol[:, 0, k:k + 1], bias=lncol[:, 1, k:k + 1]),
                                     [PB[pb], blncol], [bh1T[t]])
                            else:
                                T.op("dve", lambda: nc.vector.tensor_scalar(out=h1T[:, k, t * 128:(t + 1) * 128], in0=PS[pb][:, j * 128:(j + 1) * 128],
                                                                           scalar1=lncol[:, 0, k:k + 1], scalar2=lncol[:, 1, k:k + 1],
                                                                           op0=ALU.mult, op1=ALU.add), [PB[pb], blncol], [bh1T2[t]])
                    T.op("pool", lambda: nc.gpsimd.tensor_tensor(out=xtm[:, t, :], in0=xtm[:, t, :], in1=lnp[:, 0, :], op=ALU.mult),
                         [bxtm[t], blnp], [bxtm[t]])
                    T.op("pool", lambda: nc.gpsimd.tensor_tensor(out=xtm[:, t, :], in0=xtm[:, t, :], in1=lnp[:, 1, :], op=ALU.add),
                         [bxtm[t], blnp], [bxtm[t]])

            def rest(mt):
                for f0 in range(0, 22, 4):
                    nfl = min(4, 22 - f0)
                    wg_ = wload(wgb, 0, 8, f0 * 128, nfl * 128)
                    wu_ = wload(wub, 0, 8, f0 * 128, nfl * 128)
                    for fl in range(nfl):
                        cs = slice(fl * 128, (fl + 1) * 128)
                        pg = rp.next(); pu = rp.next()
                        for (pp, wx) in [(pg, wg_), (pu, wu_)]:
                            for k in range(8):
                                T.op("pe", lambda: nc.tensor.matmul(PS[pp][:], lhsT=WP[wx][:, k, cs], rhs=h1T[:, k, :],
                                                                   start=(k == 0), stop=(k == 7)), [bWP[wx]] + bh1T + bh1T2, [PB[pp]], inc=(k == 7))
                        a_ = rA.next()
                        T.op("act", lambda: nc.scalar.activation(out=tA[a_][:], in_=PS[pg][:], func=AF.Silu), [PB[pg]], [btA[a_]])
                        T.op("dve", lambda: nc.vector.tensor_tensor(out=actT[:, f0 + fl, :], in0=tA[a_][:], in1=PS[pu][:], op=ALU.mult),
                             [btA[a_], PB[pu]], [bactT[f0 + fl]])
                for n in range(2):
                    wd_ = [wload(wdb, k0, nk, n * 512, 512) for (k0, nk) in [(0, 8), (8, 8), (16, 6)]]
                    for t in range(4):
                        pb = rp.next()
                        for k in range(22):
                            T.op("pe", lambda: nc.tensor.matmul(PS[pb][:], lhsT=actT[:, k, t * 128:(t + 1) * 128], rhs=WP[wd_[k // 8]][:, k % 8, :],
                                                               start=(k == 0), stop=(k == 21)), [bactT[k], bWP[wd_[k // 8]]], [PB[pb]], inc=(k == 21))
                        T.op("dve", lambda: nc.vector.scalar_tensor_tensor(out=acc[:, t, n * 512:(n + 1) * 512], in0=xtm[:, t, n * 512:(n + 1) * 512],
                                                                          scalar=ALPHA, in1=PS[pb][:], op0=ALU.mult, op1=ALU.add),
                             [bxtm[t], PB[pb]], [bacc[t]])
                wpg_ = [wload(wpgb, 0, 8, n * 512, 512) for n in range(2)]
                wpl_ = [wload(wpleb, 0, 2, n * 512, 512) for n in range(2)]
                for t in range(4):
                    for n in range(2):
                        pg = rp.next(); pl = rp.next()
                        for k in range(8):
                            T.op("pe", lambda: nc.tensor.matmul(PS[pg][:], lhsT=h1T[:, k, t * 128:(t + 1) * 128], rhs=WP[wpg_[n]][:, k, :],
                                                               start=(k == 0), stop=(k == 7)), [bh1T[t], bh1T2[t], bWP[wpg_[n]]], [PB[pg]], inc=(k == 7))
                        for k in range(2):
                            T.op("pe", lambda: nc.tensor.matmul(PS[pl][:], lhsT=pTs[:, k, t * 128:(t + 1) * 128], rhs=WP[wpl_[n]][:, k, :],
                                                               start=(k == 0), stop=(k == 1)), [bpT, bWP[wpl_[n]]], [PB[pl]], inc=(k == 1))
                        a_ = rA.next()
                        T.op("act", lambda: nc.scalar.activation(out=tA[a_][:], in_=PS[pg][:], func=AF.Sigmoid), [PB[pg]], [btA[a_]])
                        T.op("dve", lambda: nc.vector.tensor_tensor(out=tA[a_][:], in0=tA[a_][:], in1=PS[pl][:], op=ALU.mult),
                             [btA[a_], PB[pl]], [btA[a_]])
                        dsl = acc[:, t, n * 512:(n + 1) * 512]
                        T.op("dve", lambda: nc.vector.tensor_tensor(out=dsl, in0=dsl, in1=tA[a_][:], op=ALU.add), [bacc[t], btA[a_]], [bacc[t]])
                    ln_stats(acc[:, t, :], bacc[t], 4 + t)
                    o = rO.next()
                    T.op("pool", lambda: nc.gpsimd.tensor_tensor(out=outs[o][:], in0=acc[:, t, :], in1=lnp[:, 2, :], op=ALU.mult),
                         [bacc[t], blnp], [bouts[o]])
                    T.op("pool", lambda: nc.gpsimd.tensor_tensor(out=outs[o][:], in0=outs[o][:], in1=lnp[:, 3, :], op=ALU.add),
                         [bouts[o], blnp], [bouts[o]])
                    r0 = mt * 512 + t * 128
                    T.dma("pool", out_d[s, r0:r0 + 128, :], outs[o][:], [bouts[o]], [D["out"]], bouts[o])

            gate_step()
            if len(mlist) > 1:
                load_gate_inputs(mlist[1])
            for mi, mt in enumerate(mlist):
                mix_ln1(mt)
                if mi + 1 < len(mlist):
                    gate_step()
                    if mi + 2 < len(mlist):
                        load_gate_inputs(mlist[mi + 2])
                rest(mt)
        T.barrier()

    prep_small()
    prep_big()
    if "A" in phases:
        phase_a(list(range(nseq)))
    for s in range(nseq):
        if "B" in phases:
            phase_b(s)
        prep_emit(len(prep_items))
        if "C" in phases:
            phase_c(s)
    T.barrier()
    T._wait("sp", [(k, v) for k, v in T.cnt.items() if v > 0])
    build_nc.stats = (T.n_inst, T.n_wait, len(T.sems))
    return nc


def _prep_inputs(inputs):
    f = lambda a: np.ascontiguousarray(np.asarray(a))
    x = f(inputs["x"]); p = f(inputs["p"])[0]; pos = f(inputs["positions"])
    w_in = f(inputs["w_in"])[0]
    cols = _wa_columns()
    shared = {
        "wa": np.ascontiguousarray(w_in[:, cols]),
        "wgm": np.ascontiguousarray(w_in[:, 1944:3992]),
        "sinks": f(inputs["attn_sinks"]).reshape(1, 8),
        "wpa": f(inputs["w_proj_swa"])[0], "wpb": f(inputs["w_proj_nsa"])[0], "wout": f(inputs["w_out"])[0],
        "wg": f(inputs["w_ff_gate"])[0], "wu": f(inputs["w_ff_up"])[0], "wd": f(inputs["w_ff_down"])[0],
        "wple": f(inputs["w_ple"])[0], "wpg": f(inputs["w_ple_gate"])[0],
        "ln": np.ascontiguousarray(np.stack([f(inputs["ln1_g"])[0], f(inputs["ln1_b"])[0], f(inputs["ln2_g"])[0], f(inputs["ln2_b"])[0]])),
        "w2k": f(inputs["w_cmp_k2"])[0], "w2v": f(inputs["w_cmp_v2"])[0],
    }
    shared["lncol"] = np.ascontiguousarray(np.stack([f(inputs["ln1_g"])[0].reshape(8, 128).T, f(inputs["ln1_b"])[0].reshape(8, 128).T], axis=1))
    pe = f(inputs["cmp_pos_emb"])[0]
    peT = np.transpose(pe, (2, 0, 1))
    shared["peT"] = np.ascontiguousarray(np.concatenate([peT[:, :, :16], peT[:, :, 16:]], axis=0))
    for nm, key in [("w1k", "w_cmp_k1"), ("w1v", "w_cmp_v1")]:
        w1 = f(inputs[key])[0].reshape(32, 64, 256).transpose(1, 0, 2)
        shared[nm] = np.ascontiguousarray(np.concatenate([w1[:, :16], w1[:, 16:]], axis=0))
    for k, v in _consts().items():
        shared["c_" + k] = v
    in_maps = []
    for c in range(8):
        sl = slice(c * NSEQ, (c + 1) * NSEQ)
        m = dict(shared)
        m["x"] = np.ascontiguousarray(x[sl])
        m["xT"] = np.ascontiguousarray(np.transpose(x[sl], (0, 2, 1)))
        m["pT"] = np.ascontiguousarray(np.transpose(p[sl], (0, 2, 1)))
        m["pos"] = np.ascontiguousarray(pos[sl].astype(np.int32))
        in_maps.append(m)
    return in_maps


def kernel(**inputs):
    in_maps = _prep_inputs(inputs)
    nc = build_nc()
    res = run_bass_kernel_spmd(nc, in_maps, core_ids=list(range(8)))
    out = np.concatenate([np.asarray(r["out"]) for r in res.results], axis=0)
    return out.astype(np.float32)
```

```python
import math
from contextlib import ExitStack

import numpy as np
import ml_dtypes
import concourse.bass as bass
import concourse.mybir as mybir
from concourse.bass_utils import run_bass_kernel_spmd

F32 = mybir.dt.float32
BF16 = mybir.dt.bfloat16
I32 = mybir.dt.int32
AF = mybir.ActivationFunctionType
ALU = mybir.AluOpType

S = 4096
DM = 1024
NSEQ = 2
DFF = 2816
NEGB = -30000.0
ALPHA = 2.0 ** 0.25
NWA = 3416
DBG = {}
DEBUG_OUT = ("QT_A", "QT_R", "QT_N", "KT_A", "KT_S", "KT_W", "KT_C", "VT_C", "VTM", "EGN", "KCT_D", "VCA_D", "OT_A", "OT_B", "wab")


class Buf:
    __slots__ = ("name", "w", "r", "sem", "excl")

    def __init__(self, name, excl=False):
        self.name = name
        self.w = {}
        self.r = {}
        self.sem = None
        self.excl = excl


class Trk:
    def __init__(self, nc, stack):
        self.nc = nc
        self.stack = stack
        self.eng = {"pe": nc.tensor, "act": nc.scalar, "dve": nc.vector, "pool": nc.gpsimd, "sp": nc.sync}
        self.sems = {}
        self.cnt = {}
        self.seen = {e: {} for e in self.eng}
        self.pending = {e: ([], []) for e in self.eng}
        for e in self.eng:
            self._sem("E_" + e)
        self.n_inst = 0
        self.n_wait = 0

    def _sem(self, key):
        if key not in self.sems:
            self.sems[key] = self.stack.enter_context(self.nc.semaphore(key))
            self.cnt[key] = 0
        return self.sems[key]

    def _wait(self, e, conds):
        seen = self.seen[e]
        need = {}
        for (k, v) in conds:
            if seen.get(k, 0) >= v:
                continue
            if need.get(k, 0) < v:
                need[k] = v
        for k, v in need.items():
            self.eng[e].wait_ge(self.sems[k], v)
            seen[k] = v
            self.n_wait += 1

    def op(self, e, fn, reads=(), writes=(), inc=True):
        conds = []
        own = "E_" + e
        writes = list(writes) + [b for b in reads if b.excl]
        reads = [b for b in reads if not b.excl]
        for b in reads:
            conds += list(b.w.items())
        for b in writes:
            conds += list(b.w.items())
            conds += list(b.r.items())
        if e == "pe":
            conds = [c for c in conds if c[0] != own]
        self._wait(e, conds)
        ins = fn()
        self.n_inst += 1
        pr, pw = self.pending[e]
        pr += list(reads)
        pw += list(writes)
        if inc:
            self.cnt[own] += 1
            v = self.cnt[own]
            ins.then_inc(self.sems[own], 1)
            for b in pr:
                b.r[own] = v
            for b in pw:
                b.w = {own: v}
                b.r = {}
            self.pending[e] = ([], [])
        return ins

    def dma(self, q, out, in_, reads, writes, sembuf, **kw):
        conds = []
        for b in reads:
            conds += list(b.w.items())
        for b in writes:
            conds += list(b.w.items())
            conds += list(b.r.items())
        if sembuf.sem is None:
            sembuf.sem = "D_" + sembuf.name
        self._sem(sembuf.sem)
        k = sembuf.sem
        conds = [c for c in conds if c[0] != k]
        self._wait(q, conds)
        self.cnt[k] += 16
        v = self.cnt[k]
        ins = self.eng[q].dma_start(out=out, in_=in_, **kw)
        ins.then_inc(self.sems[k], 16)
        self.n_inst += 1
        for b in reads:
            b.r[k] = v
        for b in writes:
            b.w = {k: v}
            b.r = {}
        return ins

    def barrier(self):
        allc = [(k, v) for k, v in self.cnt.items() if v > 0 and not k.startswith("D_prep")]
        for e in self.eng:
            self._wait(e, allc)


class Rot:
    def __init__(self, items):
        self.items = items
        self.i = 0

    def next(self):
        it = self.items[self.i % len(self.items)]
        self.i += 1
        return it


def _perm_half(cols):
    cols = np.asarray(cols).reshape(-1, 64)
    return np.concatenate([cols[:, 32:], cols[:, :32]], axis=1).reshape(-1)


def _wa_columns():
    qa0, ka0, va0, qn0, kc0, vc0, ks0, vs0, kw0, vw0, gn0 = 0, 512, 576, 640, 1152, 1280, 1408, 1536, 1664, 1792, 1920
    r = np.arange
    chunks = []
    qa_pairs = [np.concatenate([qa0 + j * 64 + r(64), qa0 + (4 + j) * 64 + r(64)]) for j in range(4)]
    qn_pairs = [np.concatenate([qn0 + j * 64 + r(64), qn0 + (4 + j) * 64 + r(64)]) for j in range(4)]
    chunks += qa_pairs
    chunks += [_perm_half(c) for c in qa_pairs]
    kad = np.concatenate([ka0 + r(64), ka0 + r(64)])
    chunks += [kad, _perm_half(kad)]
    chunks += qn_pairs
    chunks += [_perm_half(c) for c in qn_pairs]
    ks = ks0 + r(128)
    kw = kw0 + r(128)
    chunks += [ks, _perm_half(ks), kw, _perm_half(kw)]
    chunks += [kc0 + r(128), vc0 + r(128)]
    tm = np.concatenate([va0 + r(64), vs0 + r(128), vw0 + r(128), gn0 + r(24)])
    cols = np.concatenate(chunks + [tm])
    assert cols.shape[0] == NWA
    return cols


def _consts():
    bf = ml_dtypes.bfloat16
    c = {}
    c["ident"] = np.eye(128, dtype=np.float32).astype(bf)
    c["identf"] = np.eye(128, dtype=np.float32)
    k = np.arange(128)[:, None]
    q = np.arange(128)[None, :]
    diag = np.where(k <= q, 0.0, NEGB).astype(np.float32)
    upper = np.where(k > q, 0.0, NEGB).astype(np.float32)
    c["bdiag4"] = np.tile(diag, (1, 4)).astype(bf)
    c["bupper4"] = np.tile(upper, (1, 4)).astype(bf)
    rr = np.arange(128)[:, None]
    tbl = np.where(16 * (rr - 120) + 31 <= q, 0.0, NEGB).astype(np.float32)
    c["tbl4"] = np.tile(tbl, (1, 4)).astype(bf)
    m = np.arange(256)[None, :]
    c["wide"] = (m == rr + 8).astype(np.float32).astype(bf)
    j = np.arange(64)[:, None]
    mm = np.arange(4096)[None, :]
    eall = np.zeros((128, 4096), np.float32)
    eall[:64] = (mm // 64 == j)
    eall[64:] = eall[:64]
    c["eall"] = eall.astype(bf)
    d = np.arange(128)[None, :] - 62
    s_ = (np.arange(128)[:, None] >= 64).astype(np.int64)
    adj = np.zeros((128, 128), np.float32)
    adj = np.where(d > s_, -1e30, adj)
    adj = np.where((d == s_) | (d == s_ - 1), 1e30, adj)
    c["adjtbl"] = adj.astype(np.float32)
    cs = (np.arange(256) * 16)[:, None]
    bs = (np.arange(64) * 64)[None, :]
    ov = ((cs < bs + 64) & (cs + 32 > bs)).astype(np.float32)
    ov[255] = 0.0
    c["ov"] = ov.reshape(2, 128, 64).transpose(1, 0, 2).copy().astype(bf)
    p = np.arange(128)
    inv = (10000.0 ** (-(2.0 * (p % 32)) / 64.0)).astype(np.float32)
    sign = np.where((p % 64) < 32, -1.0, 1.0).astype(np.float32)
    c["ropec"] = np.stack([inv, sign], axis=1).astype(np.float32)
    return c


def build_nc(debug=False, nseq=NSEQ, phases="ABCD"):
    nc = bass.Bass("TRN2", target_bir_lowering=False)
    st = ExitStack()
    T = Trk(nc, st)

    def din(name, shape, dt):
        return nc.dram_tensor(name, list(shape), dt, kind="ExternalInput").ap()

    def dscr(name, shape, dt, out=False):
        return nc.dram_tensor(name, list(shape), dt, kind=("ExternalOutput" if (out or (debug and name.rstrip("01") in DEBUG_OUT)) else "Internal")).ap()

    xT_in = din("xT", [NSEQ, DM, S], F32)
    x_in = din("x", [NSEQ, S, DM], F32)
    pT_in = din("pT", [NSEQ, 256, S], F32)
    pos_in = din("pos", [NSEQ, S], I32)
    wa_in = din("wa", [DM, NWA], F32)
    wgm_in = din("wgm", [DM, 2048], F32)
    sinks_in = din("sinks", [1, 8], F32)
    peT_in = din("peT", [128, 2, 16], F32)
    w1k_in = din("w1k", [128, 16, 256], F32)
    w1v_in = din("w1v", [128, 16, 256], F32)
    w2k_in = din("w2k", [256, 64], F32)
    w2v_in = din("w2v", [256, 64], F32)
    wpa_in = din("wpa", [512, DM], F32)
    wpb_in = din("wpb", [512, DM], F32)
    wout_in = din("wout", [DM, DM], F32)
    wg_in = din("wg", [DM, DFF], F32)
    wu_in = din("wu", [DM, DFF], F32)
    wd_in = din("wd", [DFF, DM], F32)
    wple_in = din("wple", [256, DM], F32)
    wpg_in = din("wpg", [DM, DM], F32)
    ln_in = din("ln", [4, DM], F32)
    c_ident = din("c_ident", [128, 128], BF16)
    c_bdiag4 = din("c_bdiag4", [128, 512], BF16)
    c_bupper4 = din("c_bupper4", [128, 512], BF16)
    c_tbl4 = din("c_tbl4", [128, 512], BF16)
    c_wide = din("c_wide", [128, 256], BF16)
    c_eall = din("c_eall", [128, 4096], BF16)
    c_adjtbl = din("c_adjtbl", [128, 128], F32)
    c_ov = din("c_ov", [128, 2, 64], BF16)
    c_ropec = din("c_ropec", [128, 2], F32)
    c_identf = din("c_identf", [128, 128], F32)
    lncol_in = din("lncol", [128, 2, 8], F32)

    out_d = nc.dram_tensor("out", [NSEQ, S, DM], F32, kind="ExternalOutput").ap()

    xTb = dscr("xTb", [NSEQ, DM, S], BF16)
    pTb = dscr("pTb", [NSEQ, 256, S], BF16)
    wab = dscr("wab", [DM, NWA], BF16)
    wgmb = dscr("wgmb", [DM, 2048], BF16)
    w1kb = dscr("w1kb", [128, 16, 256], BF16)
    w1vb = dscr("w1vb", [128, 16, 256], BF16)
    w2kb = dscr("w2kb", [256, 64], BF16)
    w2vb = dscr("w2vb", [256, 64], BF16)
    peTb = dscr("peTb", [128, 2, 16], BF16)
    wpab = dscr("wpab", [512, DM], BF16)
    wpbb = dscr("wpbb", [512, DM], BF16)
    woutb = dscr("woutb", [DM, DM], BF16)
    wgb = dscr("wgb", [DM, DFF], BF16)
    wub = dscr("wub", [DM, DFF], BF16)
    wdb = dscr("wdb", [DFF, DM], BF16)
    wpleb = dscr("wpleb", [256, DM], BF16)
    wpgb = dscr("wpgb", [DM, DM], BF16)

    SCR = []
    DD = []
    for q_ in range(NSEQ):
        sc = {}
        sc["QT_A"] = dscr("QT_A%d" % q_, [128, 4, S], BF16)
        sc["QT_R"] = dscr("QT_R%d" % q_, [128, 4, S], BF16)
        sc["QT_N"] = dscr("QT_N%d" % q_, [128, 4, S], BF16)
        sc["KT_A"] = dscr("KT_A%d" % q_, [128, S], BF16)
        sc["KT_S"] = dscr("KT_S%d" % q_, [128, S], BF16)
        sc["KT_W"] = dscr("KT_W%d" % q_, [128, S], BF16)
        sc["KT_C"] = dscr("KT_C%d" % q_, [128, S], BF16)
        sc["VT_C"] = dscr("VT_C%d" % q_, [128, S], BF16)
        sc["VTM"] = dscr("VTM%d" % q_, [S, 320], BF16)
        sc["EGN"] = dscr("EGN%d" % q_, [S, 24], F32)
        sc["KCT_D"] = dscr("KCT_D%d" % q_, [128, 256], BF16)
        sc["VCA_D"] = dscr("VCA_D%d" % q_, [128, 2, 2, 64], BF16)
        sc["OT_A"] = dscr("OT_A%d" % q_, [128, 4, S], BF16)
        sc["OT_B"] = dscr("OT_B%d" % q_, [128, 4, S], BF16)
        SCR.append(sc)
        DD.append({n: Buf("d_%s%d" % (n, q_)) for n in list(sc.keys()) + ["out"]})

    def scr(q_):
        sc = SCR[q_]
        return [sc[n] for n in ['QT_A', 'QT_R', 'QT_N', 'KT_A', 'KT_S', 'KT_W', 'KT_C', 'VT_C', 'VTM', 'EGN', 'KCT_D', 'VCA_D', 'OT_A', 'OT_B']] + [DD[q_]]

    B_prepA = Buf("prepA")
    B_prepM = Buf("prepM")
    B_prepC = Buf("prepC")
    B_prepX = [Buf("prepX%d" % i) for i in range(NSEQ)]
    B_prepP = [Buf("prepP%d" % i) for i in range(NSEQ)]

    PS = [st.enter_context(nc.psum_tensor("ps%d" % i, [128, 512], F32)) for i in range(8)]
    PB = [Buf("ps%d" % i, excl=True) for i in range(8)]

    uid = [0]

    def sbuf(stack, name, shape, dt):
        uid[0] += 1
        return stack.enter_context(nc.sbuf_tensor("%s_u%d" % (name, uid[0]), list(shape), dt))

    pace = Buf("pace")
    prep_items = []

    def cast2d(dst, src, rows, buf, paced, split=128):
        for r0 in range(0, rows, split):
            r1 = min(rows, r0 + split)
            if paced:
                prep_items.append(lambda d_=dst[r0:r1], s_=src[r0:r1], b_=buf: T.dma("pool", d_, s_, [pace], [b_], b_))
            else:
                T.dma("pool", dst[r0:r1], src[r0:r1], [], [buf], buf)

    def prep_small():
        for (d_, s_, rows) in [(w1kb, w1k_in, 128), (w1vb, w1v_in, 128), (w2kb, w2k_in, 256), (w2vb, w2v_in, 256),
                               (peTb, peT_in, 128)]:
            cast2d(d_, s_, rows, B_prepM, False)

    def prep_big():
        for (d_, s_, rows) in [(wgmb, wgm_in, DM), (wpab, wpa_in, 512), (wpbb, wpb_in, 512), (woutb, wout_in, DM)]:
            cast2d(d_, s_, rows, B_prepC, True)
        cast2d(xTb[0], xT_in[0], DM, B_prepX[0], True)
        for (d_, s_, rows) in [(wgb, wg_in, DM), (wub, wu_in, DM), (wdb, wd_in, DFF), (wpleb, wple_in, 256), (wpgb, wpg_in, DM)]:
            cast2d(d_, s_, rows, B_prepC, True)
        cast2d(pTb[0], pT_in[0], 256, B_prepP[0], True)
        for s in range(1, nseq):
            cast2d(xTb[s], xT_in[s], DM, B_prepX[s], True)
            cast2d(pTb[s], pT_in[s], 256, B_prepP[s], True)

    def prep_emit(n, pace_src=None):
        if pace_src is not None:
            pace.w = dict(pace_src.w)
        for _ in range(n):
            if prep_items:
                prep_items.pop(0)()

    def phase_a(seqs):
        with ExitStack() as sk:
            WA = sbuf(sk, "A_WA", [128, 8, NWA], BF16); bWAk = [Buf("A_WA%d" % k) for k in range(8)]
            xTs = [sbuf(sk, "A_xT%d" % i, [128, 8, 512], BF16) for i in range(2)]
            bxT = [Buf("A_xT%d" % i) for i in range(2)]
            posi = [sbuf(sk, "A_pos%d" % i, [128, 512], I32) for i in range(2)]
            bpos = [Buf("A_pos%d" % i) for i in range(2)]
            ropec = sbuf(sk, "A_ropec", [128, 2], F32); bropec = Buf("A_ropec")
            halfpi = sbuf(sk, "A_halfpi", [128, 1], F32); bhalfpi = Buf("A_halfpi")
            ang = sbuf(sk, "A_ang", [128, 512], F32); bang = Buf("A_ang")
            tu = sbuf(sk, "A_tu", [128, 512], F32); btu = Buf("A_tu")
            tki = sbuf(sk, "A_tki", [128, 512], I32); btki = Buf("A_tki")
            tkf = sbuf(sk, "A_tkf", [128, 512], F32); btkf = Buf("A_tkf")
            tr = sbuf(sk, "A_tr", [128, 512], F32); btr = Buf("A_tr")
            cosT = sbuf(sk, "A_cos", [128, 512], F32); bcos = Buf("A_cos")
            sinT = sbuf(sk, "A_sin", [128, 512], F32); bsin = Buf("A_sin")
            t1 = [sbuf(sk, "A_t1_%d" % i, [128, 512], F32) for i in range(2)]
            bt1 = [Buf("A_t1_%d" % i) for i in range(2)]
            t2 = [sbuf(sk, "A_t2_%d" % i, [128, 512], F32) for i in range(2)]
            bt2 = [Buf("A_t2_%d" % i) for i in range(2)]
            stg = [sbuf(sk, "A_stg%d" % i, [128, 512], BF16) for i in range(4)]
            bstg = [Buf("A_stg%d" % i) for i in range(4)]
            vst = [sbuf(sk, "A_vst%d" % i, [128, 320], BF16) for i in range(2)]
            bvst = [Buf("A_vst%d" % i) for i in range(2)]
            est = [sbuf(sk, "A_est%d" % i, [128, 24], F32) for i in range(2)]
            best = [Buf("A_est%d" % i) for i in range(2)]
            rs = Rot(list(range(4)))
            rt = Rot(list(range(2)))
            rv = Rot(list(range(2)))
            rp = Rot(list(range(8)))

            T.dma("sp", ropec[:], c_ropec, [], [bropec], bropec)
            T.op("dve", lambda: nc.vector.memset(halfpi[:], math.pi / 2), [], [bhalfpi])
            wav = wa_in.rearrange("(k p) n -> p k n", p=128)

            def load_x(mt):
                i = mt % 2
                for k in range(8):
                    if mt == 0 and s == seqs[0]:
                        T.dma("pool", WA[:, k, :], wav[:, k, :], [], [bWAk[k]], bWAk[k])
                    T.dma("pool", xTs[i][:, k, :], xv[:, k, mt * 512:(mt + 1) * 512], [], [bxT[i]], bxT[i])
                T.dma("sp", posi[i][:], pos_in[s:s + 1, mt * 512:(mt + 1) * 512].to_broadcast([128, 512]), [], [bpos[i]], bpos[i])

            C1 = 6.28125
            C2 = 2.0 * math.pi - 6.28125

            def trig(dst, bdst, shift, use_sign):
                T.op("dve", lambda: nc.vector.tensor_scalar(out=tu[:], in0=ang[:], scalar1=1.0 / (2 * math.pi),
                                                           scalar2=shift / (2 * math.pi), op0=ALU.mult, op1=ALU.add),
                     [bang], [btu])
                T.op("dve", lambda: nc.vector.tensor_copy(out=tki[:], in_=tu[:]), [btu], [btki])
                T.op("dve", lambda: nc.vector.tensor_copy(out=tkf[:], in_=tki[:]), [btki], [btkf])
                T.op("dve", lambda: nc.vector.scalar_tensor_tensor(out=tr[:], in0=tkf[:], scalar=-C1, in1=ang[:],
                                                                  op0=ALU.mult, op1=ALU.add), [btkf, bang], [btr])
                T.op("dve", lambda: nc.vector.scalar_tensor_tensor(out=tu[:], in0=tkf[:], scalar=-C2, in1=tr[:],
                                                                  op0=ALU.mult, op1=ALU.add), [btkf, btr], [btu])
                lim = math.pi - shift - 1e-5
                T.op("dve", lambda: nc.vector.tensor_scalar(out=tu[:], in0=tu[:], scalar1=-math.pi - shift + 1e-5,
                                                           scalar2=lim, op0=ALU.max, op1=ALU.min), [btu], [btu])
                if shift != 0.0:
                    T.op("act", lambda: nc.scalar.activation(out=dst[:], in_=tu[:], func=AF.Sin, bias=halfpi[:, 0:1]),
                         [btu, bhalfpi], [bdst])
                else:
                    T.op("act", lambda: nc.scalar.activation(out=tr[:], in_=tu[:], func=AF.Sin), [btu], [btr])
                    T.op("dve", lambda: nc.vector.tensor_scalar(out=dst[:], in0=tr[:], scalar1=ropec[:, 1:2], scalar2=None,
                                                               op0=ALU.mult), [btr, bropec], [bdst])

            for s in seqs:
                (QT_A, QT_R, QT_N, KT_A, KT_S, KT_W, KT_C, VT_C, VTM, EGN, KCT_D, VCA_D, OT_A, OT_B, D) = scr(s)
                xv = xT_in[s].rearrange("(k p) n -> p k n", p=128)
                load_x(0)
                for mt in range(8 if DBG.get("a_mt") is None else DBG["a_mt"]):
                    i = mt % 2
                    if mt + 1 < 8:
                        load_x(mt + 1)
                    tok = slice(mt * 512, (mt + 1) * 512)
                    T.op("dve", lambda: nc.vector.tensor_copy(out=tkf[:], in_=posi[i][:]), [bpos[i]], [btkf])
                    T.op("dve", lambda: nc.vector.tensor_scalar(out=ang[:], in0=tkf[:], scalar1=ropec[:, 0:1], scalar2=None,
                                                               op0=ALU.mult), [btkf, bropec], [bang])
                    trig(sinT, bsin, 0.0, True)
                    trig(cosT, bcos, math.pi / 2, False)
                    if DBG.get("a_stop") == 1:
                        continue

                    def fm(chunk, pb):
                        for k in range(8):
                            T.op("pe", lambda: nc.tensor.matmul(PS[pb][:], lhsT=WA[:, k, chunk * 128:(chunk + 1) * 128],
                                                               rhs=xTs[i][:, k, :], start=(k == 0), stop=(k == 7)),
                                 [bWAk[k], bxT[i]], [PB[pb]], inc=(k == 7))

                    def roped(cz, cp, dst_ap, dbuf, extra=None):
                        pz = rp.next(); pp = rp.next()
                        fm(cz, pz); fm(cp, pp)
                        a = rt.next()
                        T.op("dve", lambda: nc.vector.tensor_tensor(out=t1[a][:], in0=PS[pz][:], in1=cosT[:], op=ALU.mult),
                             [PB[pz], bcos], [bt1[a]])
                        T.op("dve", lambda: nc.vector.tensor_tensor(out=t2[a][:], in0=PS[pp][:], in1=sinT[:], op=ALU.mult),
                             [PB[pp], bsin], [bt2[a]])
                        g = rs.next()
                        T.op("dve", lambda: nc.vector.tensor_tensor(out=stg[g][:], in0=t1[a][:], in1=t2[a][:], op=ALU.add),
                             [bt1[a], bt2[a]], [bstg[g]])
                        T.dma("sp", dst_ap, stg[g][:], [bstg[g]], [dbuf], bstg[g])
                        if extra is not None:
                            g2 = rs.next()
                            T.op("act", lambda: nc.scalar.copy(out=stg[g2][:], in_=PS[pz][:]), [PB[pz]], [bstg[g2]])
                            T.dma("sp", extra[0], stg[g2][:], [bstg[g2]], [extra[1]], bstg[g2])

                    def plain(cz, dst_ap, dbuf):
                        pz = rp.next()
                        fm(cz, pz)
                        g = rs.next()
                        T.op("act", lambda: nc.scalar.copy(out=stg[g][:], in_=PS[pz][:]), [PB[pz]], [bstg[g]])
                        T.dma("sp", dst_ap, stg[g][:], [bstg[g]], [dbuf], bstg[g])

                    for j in range(4):
                        roped(j, 4 + j, QT_A[:, j, tok], D["QT_A"])
                        if DBG.get("a_stop") == 2:
                            break
                    if DBG.get("a_stop") == 2:
                        continue
                    roped(8, 9, KT_A[:, tok], D["KT_A"])
                    if DBG.get("a_stop") == 4:
                        continue
                    for j in range(4):
                        roped(10 + j, 14 + j, QT_R[:, j, tok], D["QT_R"], extra=(QT_N[:, j, tok], D["QT_N"]))
                    if DBG.get("a_stop") == 5:
                        continue
                    roped(18, 19, KT_S[:, tok], D["KT_S"])
                    roped(20, 21, KT_W[:, tok], D["KT_W"])
                    plain(22, KT_C[:, tok], D["KT_C"])
                    plain(23, VT_C[:, tok], D["VT_C"])
                    if DBG.get("a_stop") == 3:
                        continue
                    for t in range(4):
                        pb = rp.next()
                        for k in range(8):
                            T.op("pe", lambda: nc.tensor.matmul(PS[pb][:, 0:344], lhsT=xTs[i][:, k, t * 128:(t + 1) * 128],
                                                               rhs=WA[:, k, 3072:3416], start=(k == 0), stop=(k == 7)),
                                 [bWAk[k], bxT[i]], [PB[pb]], inc=(k == 7))
                        v = rv.next()
                        T.op("dve", lambda: nc.vector.tensor_copy(out=vst[v][:], in_=PS[pb][:, 0:320]), [PB[pb]], [bvst[v]])
                        T.op("act", lambda: nc.scalar.activation(out=est[v][:], in_=PS[pb][:, 320:344], func=AF.Exp, scale=-1.0),
                             [PB[pb]], [best[v]])
                        r0 = mt * 512 + t * 128
                        T.dma("sp", VTM[r0:r0 + 128, :], vst[v][:], [bvst[v]], [D["VTM"]], bvst[v])
                        T.dma("sp", EGN[r0:r0 + 128, :], est[v][:], [best[v]], [D["EGN"]], best[v])
        T.barrier()

    def phase_cmp(s):
        (QT_A, QT_R, QT_N, KT_A, KT_S, KT_W, KT_C, VT_C, VTM, EGN, KCT_D, VCA_D, OT_A, OT_B, D) = scr(s)
        with ExitStack() as sk:
            XC = [[sbuf(sk, "M_XC%d_%d" % (i, g), [128, S], BF16) for g in range(2)] for i in range(2)]
            bXC = [Buf("M_XC%d" % i) for i in range(2)]
            W1 = [sbuf(sk, "M_W1_%d" % i, [128, 16, 256], BF16) for i in range(2)]
            bW1 = [Buf("M_W1_%d" % i) for i in range(2)]
            W2 = [sbuf(sk, "M_W2_%d" % i, [128, 2, 64], BF16) for i in range(2)]
            bW2 = Buf("M_W2")
            peT = sbuf(sk, "M_peT", [128, 2, 16], BF16); bpeT = Buf("M_peT")
            bias = sbuf(sk, "M_bias", [128, 4], F32); bbias = Buf("M_bias")
            H = sbuf(sk, "M_H", [128, 2, 2, 2, 256], BF16)
            bH = [[Buf("M_H%d%d" % (a, b)) for b in range(2)] for a in range(2)]
            xs = sbuf(sk, "M_xs", [128, 256], F32); bxs = Buf("M_xs")
            xu = sbuf(sk, "M_xu", [128, 256], F32); bxu = Buf("M_xu")
            xg = sbuf(sk, "M_xg", [128, 256], F32); bxg = Buf("M_xg")
            kct = sbuf(sk, "M_kct", [128, 256], BF16); bkct = Buf("M_kct")
            vca = sbuf(sk, "M_vca", [128, 2, 2, 64], BF16); bvca = Buf("M_vca")
            rp = Rot(list(range(8)))

            for kv, (src, dn_) in enumerate([(KT_C, "KT_C"), (VT_C, "VT_C")]):
                for g in range(2):
                    gs_ = slice(g * 64, (g + 1) * 64)
                    T.op("pool", lambda: nc.gpsimd.memset(XC[kv][g][64:128, S - 16:S], 0.0), [], [bXC[kv]])
                    T.dma("sp", XC[kv][g][0:64, :], src[gs_, :], [D[dn_]], [bXC[kv]], bXC[kv])
                    T.dma("sp", XC[kv][g][64:128, 0:S - 16], src[gs_, 16:S], [D[dn_]], [bXC[kv]], bXC[kv])
            T.dma("sp", W1[0][:], w1kb, [B_prepM], [bW1[0]], bW1[0])
            T.dma("sp", W1[1][:], w1vb, [B_prepM], [bW1[1]], bW1[1])
            T.dma("sp", W2[0][:], w2kb.rearrange("(k p) n -> p k n", p=128), [B_prepM], [bW2], bW2)
            T.dma("sp", W2[1][:], w2vb.rearrange("(k p) n -> p k n", p=128), [B_prepM], [bW2], bW2)
            T.dma("sp", peT[:], peTb, [B_prepM], [bpeT], bpeT)
            T.op("dve", lambda: nc.vector.memset(kct[:], 0.0), [], [bkct])
            T.op("dve", lambda: nc.vector.memset(vca[:], 0.0), [], [bvca])
            T.op("dve", lambda: nc.vector.memset(H[:], 0.0), [], [bH[0][0], bH[0][1], bH[1][0], bH[1][1]])
            pbb = rp.next()
            for kv in range(2):
                for hc in range(2):
                    col = kv * 2 + hc
                    for j in range(16):
                        T.op("pe", lambda: nc.tensor.matmul(PS[pbb][:, col:col + 1], lhsT=W1[kv][:, j, hc * 128:(hc + 1) * 128],
                                                           rhs=peT[:, kv, j:j + 1], start=(j == 0), stop=(j == 15)),
                             [bW1[kv], bpeT], [PB[pbb]], inc=(j == 15))
            T.op("dve", lambda: nc.vector.tensor_copy(out=bias[:], in_=PS[pbb][:, 0:4]), [PB[pbb]], [bbias])
            for kv in range(2):
                for g in range(2):
                    xcv = XC[kv][g][:].rearrange("p (c s) -> p c s", s=16)
                    for hc in range(2):
                        pb = rp.next()
                        for j in range(16):
                            T.op("pe", lambda: nc.tensor.matmul(
                                PS[pb][:, 0:255], lhsT=W1[kv][:, j, hc * 128:(hc + 1) * 128],
                                rhs=xcv[:, 0:255, j],
                                start=(j == 0), stop=(j == 15)), [bW1[kv], bXC[kv]], [PB[pb]], inc=(j == 15))
                        col = kv * 2 + hc
                        T.op("act", lambda: nc.scalar.activation(out=xs[:, 0:255], in_=PS[pb][:, 0:255], func=AF.Identity,
                                                                bias=bias[:, col:col + 1]), [PB[pb], bbias], [bxs])
                        T.op("dve", lambda: nc.vector.tensor_tensor(out=xu[:, 0:255], in0=xs[:, 0:255], in1=xs[:, 0:255], op=ALU.mult),
                             [bxs], [bxu])
                        T.op("dve", lambda: nc.vector.tensor_scalar(out=xu[:, 0:255], in0=xu[:, 0:255], scalar1=0.044715, scalar2=1.0,
                                                                   op0=ALU.mult, op1=ALU.add), [bxu], [bxu])
                        T.op("dve", lambda: nc.vector.tensor_tensor(out=xu[:, 0:255], in0=xu[:, 0:255], in1=xs[:, 0:255], op=ALU.mult),
                             [bxu, bxs], [bxu])
                        T.op("act", lambda: nc.scalar.activation(out=xg[:, 0:255], in_=xu[:, 0:255], func=AF.Sigmoid,
                                                                scale=2.0 * math.sqrt(2.0 / math.pi)), [bxu], [bxg])
                        T.op("dve", lambda: nc.vector.tensor_tensor(out=H[:, kv, g, hc, 0:255], in0=xg[:, 0:255], in1=xs[:, 0:255],
                                                                   op=ALU.mult), [bxg, bxs], [bH[kv][g]])
            pk = rp.next()
            for g in range(2):
                for hc in range(2):
                    T.op("pe", lambda: nc.tensor.matmul(PS[pk][g * 64:(g + 1) * 64, 0:256], lhsT=W2[0][:, hc, :], rhs=H[:, 0, g, hc, :],
                                                       start=(hc == 0), stop=(hc == 1)), [bW2, bH[0][g]], [PB[pk]], inc=(hc == 1))
            T.op("dve", lambda: nc.vector.tensor_copy(out=kct[:, 0:255], in_=PS[pk][:, 0:255]), [PB[pk]], [bkct])
            T.dma("sp", KCT_D, kct[:], [bkct], [D["KCT_D"]], bkct)
            pv = rp.next()
            for cc in range(2):
                rows = 128 if cc == 0 else 127
                for g in range(2):
                    for hc in range(2):
                        T.op("pe", lambda: nc.tensor.matmul(PS[pv][0:rows, (cc * 2 + g) * 64:(cc * 2 + g + 1) * 64],
                                                           lhsT=H[:, 1, g, hc, cc * 128:cc * 128 + rows], rhs=W2[1][:, hc, :],
                                                           start=(hc == 0), stop=(hc == 1)), [bW2, bH[1][g]], [PB[pv]], inc=(hc == 1))
            T.op("dve", lambda: nc.vector.tensor_copy(out=vca[:, 0, :, :], in_=PS[pv][:, 0:128].rearrange("p (g d) -> p g d", g=2)),
                 [PB[pv]], [bvca])
            T.op("dve", lambda: nc.vector.tensor_copy(out=vca[0:127, 1, :, :], in_=PS[pv][0:127, 128:256].rearrange("p (g d) -> p g d", g=2)),
                 [PB[pv]], [bvca])
            T.dma("sp", VCA_D, vca[:], [bvca], [D["VCA_D"]], bvca)

    def phase_b(s, qts=None):
        (QT_A, QT_R, QT_N, KT_A, KT_S, KT_W, KT_C, VT_C, VTM, EGN, KCT_D, VCA_D, OT_A, OT_B, D) = scr(s)
        with ExitStack() as sk:
            KST = sbuf(sk, "B_KST", [128, S], BF16); bKST = Buf("B_KST")
            VS = sbuf(sk, "B_VS", [128, 32, 2, 65], BF16); bVS = Buf("B_VS")
            KcT = sbuf(sk, "B_KcT", [128, 256], BF16); bKcT = Buf("B_KcT")
            VcA = sbuf(sk, "B_VcA", [128, 2, 2, 65], BF16); bVcA = Buf("B_VcA")
            ident = sbuf(sk, "B_ident", [128, 128], BF16)
            bdiag4 = sbuf(sk, "B_bdiag4", [128, 512], BF16)
            bupper4 = sbuf(sk, "B_bupper4", [128, 512], BF16)
            tbl4 = sbuf(sk, "B_tbl4", [128, 512], BF16)
            wide = sbuf(sk, "B_wide", [128, 256], BF16)
            KSE = [sbuf(sk, "B_KSE%d" % g, [128, S], BF16) for g in range(2)]
            bKSE = Buf("B_KSE")
            adjtbl = sbuf(sk, "B_adjtbl", [128, 128], F32)
            ov = sbuf(sk, "B_ov", [128, 2, 64], BF16)
            bC = Buf("B_const")
            esink = sbuf(sk, "B_esink", [128, 8], F32); besink = Buf("B_esink")
            NR = 3
            QAZ = [[sbuf(sk, "B_QAZ%d_%d" % (i, h), [128, 4, 128], BF16) for h in range(2)] for i in range(NR)]
            QNZ = [[sbuf(sk, "B_QNZ%d_%d" % (i, h), [128, 4, 128], BF16) for h in range(2)] for i in range(NR)]
            QZ = [[sbuf(sk, "B_QZ%d_%d" % (i, h), [128, 4, 128], BF16) for h in range(2)] for i in range(NR)]
            QB = [[sbuf(sk, "B_QB%d_%d" % (i, h), [128, 4, 128], BF16) for h in range(2)] for i in range(NR)]
            bQB = [[Buf("B_QB%d_%d" % (i, h)) for h in range(2)] for i in range(NR)]
            KAw = [sbuf(sk, "B_KAw%d" % i, [128, 256], BF16) for i in range(NR)]
            KWw = [sbuf(sk, "B_KWw%d" % i, [128, 640], BF16) for i in range(NR)]
            VAw = [sbuf(sk, "B_VAw%d" % i, [128, 2, 65], BF16) for i in range(NR)]
            VWw = [sbuf(sk, "B_VWw%d" % i, [128, 5, 2, 65], BF16) for i in range(NR)]
            Eg = [sbuf(sk, "B_Eg%d" % i, [128, 24], F32) for i in range(NR)]
            bIn = [Buf("B_in%d" % i) for i in range(NR)]
            NP = 4
            Pt = [sbuf(sk, "B_P%d" % i, [128, 512], BF16) for i in range(NP)]
            bP = [Buf("B_P%d" % i) for i in range(NP)]
            rP = Rot(list(range(NP)))
            oa2 = [sbuf(sk, "B_oa%d" % i, [128, 512], BF16) for i in range(2)]; boa2 = [Buf("B_oa%d" % i) for i in range(2)]
            ob2 = [sbuf(sk, "B_ob%d" % i, [128, 512], BF16) for i in range(2)]; bob2 = [Buf("B_ob%d" % i) for i in range(2)]
            oTs = [sbuf(sk, "B_oTs%d" % i, [128, 4, 128], BF16) for i in range(4)]
            boTs = [Buf("B_oTs%d" % i) for i in range(4)]
            roT = Rot(list(range(4)))
            dn = sbuf(sk, "B_dn", [128, 8], F32); bdn = Buf("B_dn")
            den3 = sbuf(sk, "B_den3", [128, 4, 3], F32); bden3 = Buf("B_den3")
            coef = sbuf(sk, "B_coef", [128, 4, 3], F32); bcoef = Buf("B_coef")
            rdc = sbuf(sk, "B_rdc", [128, 4], F32); brdc = Buf("B_rdc")
            imp = sbuf(sk, "B_imp", [128, 64], F32); bimp = Buf("B_imp")
            iw = sbuf(sk, "B_iw", [128, 64], F32); biw = Buf("B_iw")
            m8 = sbuf(sk, "B_m8", [128, 8], F32); bm8 = Buf("B_m8")
            bm = sbuf(sk, "B_bm", [128, 64], BF16); bbm = Buf("B_bm")
            accS = sbuf(sk, "B_accS", [128, 3, 260], F32); bacS = Buf("B_accS")
            egS = sbuf(sk, "B_egS", [128, 12], F32)
            tmp4 = sbuf(sk, "B_tmp4", [128, 4, 64], F32); btmp = Buf("B_tmp4")
            tmp4b = sbuf(sk, "B_tmp4b", [128, 4, 64], F32); btmpb = Buf("B_tmp4b")

            S0, S1, TB, A_SWA, A_CMP, A_IMP, A_SLC, A_WIN = range(8)
            rS = Rot([S0, S1, TB])
            PTV = {b_: PS[b_][:].bitcast(BF16) for b_ in (S0, S1, TB)}

            for a in range(4):
                T.dma("sp", KST[:, a * 1024:(a + 1) * 1024], KT_S[:, a * 1024:(a + 1) * 1024], [D["KT_S"]], [bKST], bKST)
            T.op("dve", lambda: nc.vector.memset(VS[:], 1.0), [], [bVS])
            for g in range(2):
                vsv = VTM[:, 64 + g * 64:128 + g * 64].rearrange("(c p) d -> p c d", p=128)
                for a in range(4):
                    T.dma("sp", VS[:, a * 8:(a + 1) * 8, g, 0:64], vsv[:, a * 8:(a + 1) * 8, :], [D["VTM"]], [bVS], bVS)
            phase_cmp(s)
            T.dma("sp", KcT[:], KCT_D, [D["KCT_D"]], [bKcT], bKcT)
            T.op("dve", lambda: nc.vector.memset(VcA[:], 1.0), [], [bVcA])
            for cc in range(2):
                T.dma("sp", VcA[:, cc, :, 0:64], VCA_D[:, cc], [D["VCA_D"]], [bVcA], bVcA)
            for (t_, c_) in [(ident, c_ident), (bdiag4, c_bdiag4), (bupper4, c_bupper4), (tbl4, c_tbl4), (wide, c_wide),
                             (adjtbl, c_adjtbl), (ov, c_ov)]:
                T.dma("sp", t_[:], c_, [], [bC], bC)
            T.dma("sp", esink[:], sinks_in.to_broadcast([128, 8]), [], [besink], besink)
            T.op("act", lambda: nc.scalar.activation(out=esink[:], in_=esink[:], func=AF.Exp), [besink], [besink])
            for g in range(2):
                gs_ = slice(g * 64, (g + 1) * 64)
                ot_ = slice((1 - g) * 64, (2 - g) * 64)
                for a in range(4):
                    T.dma("sp", KSE[g][gs_, a * 1024:(a + 1) * 1024], KT_S[gs_, a * 1024:(a + 1) * 1024], [D["KT_S"]], [bKSE], bKSE)
                    T.dma("sp", KSE[g][ot_, a * 1024:(a + 1) * 1024], c_eall[ot_, a * 1024:(a + 1) * 1024], [], [bKSE], bKSE)
            for i in range(NR):
                for h in range(2):
                    for tl in (QAZ, QNZ, QZ):
                        T.op("pool", lambda: nc.gpsimd.memset(tl[i][h][:], 0.0), [], [bIn[i]])
                    T.op("pool", lambda: nc.gpsimd.memset(QB[i][h][:], 0.0), [], [bIn[i], bQB[i][h]])
                T.op("dve", lambda: nc.vector.memset(VAw[i][:], 1.0), [], [bIn[i]])
                T.op("dve", lambda: nc.vector.memset(VWw[i][:], 1.0), [], [bIn[i]])

            qlist = list(range(32)) if qts is None else qts

            def load_q(qt):
                i = qt % NR
                b = bIn[i]
                q0 = qt * 128
                for h in range(2):
                    hs_ = slice(h * 64, (h + 1) * 64)
                    T.dma("sp", QAZ[i][h][hs_], QT_A[hs_, :, q0:q0 + 128], [D["QT_A"]], [b], b)
                    T.dma("sp", QNZ[i][h][hs_], QT_N[hs_, :, q0:q0 + 128], [D["QT_N"]], [b], b)
                    T.dma("sp", QZ[i][h][hs_], QT_R[hs_, :, q0:q0 + 128], [D["QT_R"]], [b], b)
                    if qt >= 8:
                        T.dma("sp", QB[i][h][hs_], QT_R[hs_, :, q0:q0 + 128], [D["QT_R"]], [b], b)
                lo = max(0, qt - 1)
                n = qt + 1 - lo
                T.dma("sp", KAw[i][:, (2 - n) * 128:256], KT_A[:, lo * 128:(qt + 1) * 128], [D["KT_A"]], [b], b)
                T.dma("sp", VAw[i][:, 2 - n:2, 0:64],
                      VTM[lo * 128:(qt + 1) * 128, 0:64].rearrange("(c p) d -> p c d", p=128), [D["VTM"]], [b], b)
                lo = max(0, qt - 4)
                n = qt + 1 - lo
                T.dma("sp", KWw[i][:, (5 - n) * 128:640], KT_W[:, lo * 128:(qt + 1) * 128], [D["KT_W"]], [b], b)
                for g in range(2):
                    T.dma("sp", VWw[i][:, 5 - n:5, g, 0:64],
                          VTM[lo * 128:(qt + 1) * 128, 192 + g * 64:256 + g * 64].rearrange("(c p) d -> p c d", p=128), [D["VTM"]], [b], b)
                T.dma("sp", Eg[i][:], EGN[q0:q0 + 128, :], [D["EGN"]], [b], b)

            def scores(lhsT, rhs, K_bufs, extra=(), M=128):
                pb = rS.next()
                n = 1 + len(extra)
                T.op("pe", lambda: nc.tensor.matmul(PS[pb][0:M, :].rearrange("p (h q) -> p h q", h=4), lhsT=lhsT, rhs=rhs,
                                                   start=True, stop=(n == 1)), K_bufs, [PB[pb]], inc=(n == 1))
                for e_i, (l2, r2, b2) in enumerate(extra):
                    last = (e_i == len(extra) - 1)
                    o2 = PS[pb][0:M, :]
                    if len(r2.shape) == 3:
                        o2 = o2.rearrange("p (h q) -> p h q", h=4)
                    T.op("pe", lambda: nc.tensor.matmul(o2, lhsT=l2, rhs=r2, start=False, stop=last), b2, [PB[pb]], inc=last)
                p = rP.next()
                T.op("act", lambda: nc.scalar.activation(out=Pt[p][0:M, :], in_=PS[pb][0:M, :], func=AF.Exp, scale=0.125),
                     [PB[pb]], [bP[p]])
                return p

            def pv(acc, p, rhs_fn, rbufs, ncol, last, M=128):
                for h in range(4):
                    T.op("pe", lambda: nc.tensor.matmul(PS[acc][:, h * ncol:(h + 1) * ncol], lhsT=Pt[p][0:M, h * 128:(h + 1) * 128],
                                                       rhs=rhs_fn, start=False, stop=last, skip_group_check=True),
                         [bP[p]] + rbufs, [PB[acc]], inc=(h == 3))

            def emit_pv(job, p):
                M = job.get("M", 128)
                for (acc, rhs_ap, rbufs, ncol) in job["pvs"]:
                    pv(acc, p, rhs_ap, rbufs, ncol, job["last"], M=M)
                if job.get("post"):
                    job["post"]()

            LA = 2

            def run_jobs(jobs):
                pend = []
                for job in jobs:
                    if job.get("pre"):
                        job["pre"]()
                    p = scores(job["lhsT"], job["rhs"], job["kb"], extra=job.get("extra", ()), M=job.get("M", 128))
                    pend.append((job, p))
                    if len(pend) > LA:
                        emit_pv(*pend.pop(0))
                while pend:
                    emit_pv(*pend.pop(0))

            def mk_memset(banks_cols):
                def f():
                    for (bk, ncols) in banks_cols:
                        T.op("dve", lambda: nc.vector.memset(PS[bk][:, 0:ncols], 0.0), [], [PB[bk]])
                return f

            def chain(*fns):
                def f():
                    for fn in fns:
                        if fn is not None:
                            fn()
                return f

            pending_tr = [None]
            pending_math = [None]
            load_q(qlist[0])
            if len(qlist) > 1:
                load_q(qlist[1])
            for qi, qt in enumerate(qlist):
                i = qt % NR
                bi = bIn[i]
                if qi + 2 < len(qlist):
                    load_q(qlist[qi + 2])
                q0 = qt * 128
                use_sel = qt >= 8
                oa, boa, ob, bob = oa2[qi % 2], boa2[qi % 2], ob2[qi % 2], bob2[qi % 2]

                def mk_swa_post(half, acc):
                    def f():
                        accv = PS[acc][:, 0:260].rearrange("p (h c) -> p h c", c=65)
                        T.op("dve", lambda: nc.vector.tensor_tensor(out=dn[:, 0:4], in0=accv[:, :, 64], in1=esink[:, half * 4:(half + 1) * 4],
                                                                   op=ALU.add), [PB[acc], besink], [bdn])
                        T.op("dve", lambda: nc.vector.reciprocal(out=dn[:, 4:8], in_=dn[:, 0:4]), [bdn], [bdn])
                        T.op("dve", lambda: nc.vector.tensor_tensor(out=oa[:, half * 256:(half + 1) * 256].rearrange("p (h d) -> p h d", h=4),
                                                                   in0=accv[:, :, 0:64], in1=dn[:, 4:8].unsqueeze(2).to_broadcast([128, 4, 64]),
                                                                   op=ALU.mult), [PB[acc], bdn], [boa])
                    return f

                def mk_topk(g):
                    oth = slice((1 - g) * 64, (2 - g) * 64)

                    def f():
                        cv = PS[A_CMP][:, 0:260].rearrange("p (h c) -> p h c", c=65)
                        T.op("dve", lambda: nc.vector.tensor_scalar(out=rdc[:], in0=cv[:, :, 64], scalar1=1e-30, scalar2=None,
                                                                   op0=ALU.max), [PB[A_CMP]], [brdc])
                        T.op("dve", lambda: nc.vector.reciprocal(out=rdc[:], in_=rdc[:]), [brdc], [brdc])
                        T.op("dve", lambda: nc.vector.tensor_scalar(out=imp[:], in0=PS[A_IMP][:, 0:64], scalar1=rdc[:, 0:1], scalar2=None,
                                                                   op0=ALU.mult), [PB[A_IMP], brdc], [bimp])
                        for h in range(1, 4):
                            T.op("dve", lambda: nc.vector.scalar_tensor_tensor(out=imp[:], in0=PS[A_IMP][:, h * 64:(h + 1) * 64],
                                                                              scalar=rdc[:, h:h + 1], in1=imp[:], op0=ALU.mult, op1=ALU.add),
                                 [PB[A_IMP], brdc, bimp], [bimp])
                        a0 = 62 - 2 * qt
                        T.op("dve", lambda: nc.vector.tensor_tensor(out=iw[:], in0=imp[:], in1=adjtbl[:, a0:a0 + 64], op=ALU.add),
                             [bimp, bC], [biw])
                        T.op("dve", lambda: nc.vector.memset(iw[:, 0:1], 1e30), [], [biw])
                        T.op("dve", lambda: nc.vector.max(out=m8[:], in_=iw[:]), [biw], [bm8])
                        T.op("dve", lambda: nc.vector.match_replace(out=iw[:], in_to_replace=m8[:], in_values=iw[:], imm_value=-3e30),
                             [biw, bm8], [biw])
                        T.op("dve", lambda: nc.vector.max(out=m8[:], in_=iw[:]), [biw], [bm8])
                        T.op("dve", lambda: nc.vector.match_replace(out=iw[:], in_to_replace=m8[:], in_values=iw[:], imm_value=-3e30),
                             [biw, bm8], [biw])
                        T.op("dve", lambda: nc.vector.tensor_scalar(out=bm[:], in0=iw[:], scalar1=-2e30, scalar2=NEGB,
                                                                   op0=ALU.is_ge, op1=ALU.mult), [biw], [bbm])

                    def f2():
                        tb = rS.next()
                        T.op("pe", lambda: nc.tensor.transpose(PTV[tb][oth, 0:128], bm[:], ident[:]), [bbm, bC], [PB[tb]])
                        T.op("dve", lambda: nc.vector.tensor_copy(out=QB[i][g][oth, :, :],
                                                                 in_=PTV[tb][oth, 0:128].unsqueeze(1).to_broadcast([64, 4, 128])),
                             [PB[tb]], [bQB[i][g]])
                    return f, f2

                def mk_combine(g, i=i, ob=ob, bob=bob):
                    def f_copy():
                        for br, acc in enumerate([A_CMP, A_SLC, A_WIN]):
                            T.op("dve", lambda: nc.vector.tensor_copy(out=accS[:, br, :], in_=PS[acc][:, 0:260]), [PB[acc]], [bacS])
                        T.op("dve", lambda: nc.vector.tensor_copy(out=egS[:], in_=Eg[i][:, g * 12:(g + 1) * 12]), [bIn[i]], [bacS])

                    def f_math():
                        def av(br):
                            return accS[:, br, :].rearrange("p (h c) -> p h c", c=65)
                        for br in range(3):
                            T.op("dve", lambda: nc.vector.tensor_scalar(out=den3[:, :, br], in0=av(br)[:, :, 64], scalar1=1e-30, scalar2=None,
                                                                       op0=ALU.max), [bacS], [bden3])
                        egv = egS[:].rearrange("p (h b) -> p h b", b=3)
                        T.op("dve", lambda: nc.vector.scalar_tensor_tensor(out=coef[:], in0=egv, scalar=1.0, in1=den3[:],
                                                                          op0=ALU.add, op1=ALU.mult), [bacS, bden3], [bcoef])
                        T.op("dve", lambda: nc.vector.reciprocal(out=coef[:], in_=coef[:]), [bcoef], [bcoef])

                        def cb(br):
                            return coef[:, :, br:br + 1].to_broadcast([128, 4, 64])
                        T.op("dve", lambda: nc.vector.tensor_tensor(out=tmp4[:], in0=av(0)[:, :, 0:64], in1=cb(0), op=ALU.mult), [bacS, bcoef], [btmp])
                        T.op("dve", lambda: nc.vector.tensor_tensor(out=tmp4b[:], in0=av(1)[:, :, 0:64], in1=cb(1), op=ALU.mult), [bacS, bcoef], [btmpb])
                        T.op("dve", lambda: nc.vector.tensor_tensor(out=tmp4[:], in0=tmp4[:], in1=tmp4b[:], op=ALU.add), [btmp, btmpb], [btmp])
                        T.op("dve", lambda: nc.vector.tensor_tensor(out=tmp4b[:], in0=av(2)[:, :, 0:64], in1=cb(2), op=ALU.mult), [bacS, bcoef], [btmpb])
                        T.op("dve", lambda: nc.vector.tensor_tensor(out=ob[:, g * 256:(g + 1) * 256].rearrange("p (h d) -> p h d", h=4),
                                                                   in0=tmp4[:], in1=tmp4b[:], op=ALU.add), [btmp, btmpb], [bob])
                    return f_copy, f_math

                for g in range(2):
                    gs = slice(g * 64, (g + 1) * 64)
                    half = g
                    j_cmp = []
                    cl = []
                    if qt <= 16:
                        cl.append((0, min(128, 8 * (qt + 1)), True))
                    else:
                        cl.append((0, 128, False))
                    if qt >= 16:
                        cl.append((1, 8 * (qt + 1) - 128, True))
                    for ci, (cc, M, masked) in enumerate(cl):
                        extra = []
                        if masked:
                            off = 8 + 120 + cc * 128 - 8 * qt
                            extra = [(wide[:, off:off + M], tbl4[:], [bC])]
                        j_cmp.append(dict(pre=mk_memset([(A_CMP, 260), (A_IMP, 256)]) if ci == 0 else None,
                                          lhsT=KcT[:, cc * 128:cc * 128 + M], rhs=QNZ[i][g][:, :, :], kb=[bKcT, bi], extra=extra, M=M,
                                          pvs=[(A_CMP, VcA[0:M, cc, g, :], [bVcA], 65), (A_IMP, ov[0:M, cc, :], [bC], 64)],
                                          last=(ci == len(cl) - 1)))
                    j_swa = []
                    chunks = ([0] if qt >= 1 else []) + [1]
                    for ci, c in enumerate(chunks):
                        bias_t = bupper4 if c == 0 else bdiag4
                        last = ci == len(chunks) - 1
                        j_swa.append(dict(pre=mk_memset([(A_SWA, 260)]) if ci == 0 else None,
                                          lhsT=KAw[i][:, c * 128:(c + 1) * 128], rhs=QAZ[i][half][:, :, :], kb=[bi],
                                          extra=[(ident[:], bias_t[:], [bC])],
                                          pvs=[(A_SWA, VAw[i][:, c, :], [bi], 65)], last=last,
                                          post=mk_swa_post(half, A_SWA) if last else None))
                    j_win = []
                    topk_dve, topk_pe = mk_topk(g) if use_sel else (None, None)
                    lo = max(0, qt - 4)
                    for kc in range(lo, qt + 1):
                        w = kc - (qt - 4)
                        extra = []
                        if kc == qt - 4:
                            extra.append((ident[:], bupper4[:], [bC]))
                        if kc == qt:
                            extra.append((ident[:], bdiag4[:], [bC]))
                        j_win.append(dict(pre=mk_memset([(A_WIN, 260)]) if kc == lo else None,
                                          lhsT=KWw[i][:, w * 128:(w + 1) * 128], rhs=QZ[i][g][:, :, :], kb=[bi], extra=extra,
                                          pvs=[(A_WIN, VWw[i][:, w, g, :], [bi], 65)], last=(kc == qt)))
                    j_slc = []
                    for kc in range(qt + 1):
                        extra = []
                        if kc == qt:
                            extra.append((ident[:], bdiag4[:], [bC]))
                        pre = None
                        if kc == 0:
                            pre = chain(topk_pe, mk_memset([(A_SLC, 260)]))
                        if use_sel:
                            l_, r_, kb_ = KSE[g][:, kc * 128:(kc + 1) * 128], QB[i][g][:, :, :], [bKSE, bi, bQB[i][g]]
                        else:
                            l_, r_, kb_ = KST[:, kc * 128:(kc + 1) * 128], QZ[i][g][:, :, :], [bKST, bi]
                        post = None
                        if kc == qt:
                            cmb_copy, cmb_math = mk_combine(g)
                            post = cmb_copy
                        j_slc.append(dict(pre=pre, lhsT=l_, rhs=r_, kb=kb_, extra=extra,
                                          pvs=[(A_SLC, VS[:, kc, g, :], [bVS], 65)], last=(kc == qt), post=post))
                    jobs = j_cmp + j_swa + j_win + j_slc
                    if use_sel:
                        jx = len(j_cmp) - 1 + LA + 1
                        assert jx < len(j_cmp) + len(j_swa) + len(j_win)
                        jobs[jx]["pre"] = chain(jobs[jx].get("pre"), topk_dve)
                    if pending_math[0] is not None:
                        jx = len(jobs) - len(j_slc) + min(2, len(j_slc) - 1)
                        jobs[jx]["pre"] = chain(jobs[jx].get("pre"), pending_math[0])
                        pending_math[0] = None
                    if g == 1 and pending_tr[0] is not None:
                        jx = min(4, len(jobs) - 1)
                        jobs[jx]["pre"] = chain(jobs[jx].get("pre"), pending_tr[0])
                        pending_tr[0] = None
                    run_jobs(jobs)
                    pending_math[0] = cmb_math
                def mk_tr(oa, boa, ob, bob, q0):
                    def f():
                        for (src, bsrc, dst, dname) in [(oa, boa, OT_A, "OT_A"), (ob, bob, OT_B, "OT_B")]:
                            tb = rS.next()
                            for k in range(4):
                                T.op("pe", lambda: nc.tensor.transpose(PTV[tb][:, k * 128:(k + 1) * 128], src[:, k * 128:(k + 1) * 128], ident[:]),
                                     [bsrc, bC], [PB[tb]], inc=(k == 3))
                            o = roT.next()
                            T.op("act", lambda: nc.scalar.copy(out=oTs[o][:], in_=PTV[tb][:, 0:512].rearrange("p (k t) -> p k t", k=4)),
                                 [PB[tb]], [boTs[o]])
                            T.dma("sp", dst[:, :, q0:q0 + 128], oTs[o][:], [boTs[o]], [D[dname]], boTs[o])
                    return f
                pending_tr[0] = mk_tr(oa, boa, ob, bob, q0)
                prep_emit(1 if qt < 8 else (3 if qt < 16 else 4), boa)
            if pending_math[0] is not None:
                pending_math[0]()
                pending_math[0] = None
            if pending_tr[0] is not None:
                pending_tr[0]()
                pending_tr[0] = None
        T.barrier()

    def phase_c(s, mts=None):
        (QT_A, QT_R, QT_N, KT_A, KT_S, KT_W, KT_C, VT_C, VTM, EGN, KCT_D, VCA_D, OT_A, OT_B, D) = scr(s)
        with ExitStack() as sk:
            NW = 9
            WP = [sbuf(sk, "C_W%d" % i, [128, 8, 512], BF16) for i in range(NW)]
            bWP = [Buf("C_W%d" % i) for i in range(NW)]
            rW = Rot(list(range(NW)))
            xTs = sbuf(sk, "C_xT", [128, 8, 512], BF16); bxT = Buf("C_xT")
            pTs = sbuf(sk, "C_pT", [128, 2, 512], BF16); bpT = Buf("C_pT")
            oTa = sbuf(sk, "C_oTa", [128, 4, 512], BF16); boTa = Buf("C_oTa")
            oTb = sbuf(sk, "C_oTb", [128, 4, 512], BF16); boTb = Buf("C_oTb")
            xtm = sbuf(sk, "C_xtm", [128, 4, DM], F32)
            bxtm = [Buf("C_xtm%d" % t) for t in range(4)]
            lnp = sbuf(sk, "C_lnp", [128, 4, DM], F32); blnp = Buf("C_lnp")
            lncol = sbuf(sk, "C_lncol", [128, 2, 8], F32); blncol = Buf("C_lncol")
            G = [sbuf(sk, "C_G%d" % i, [128, 2, 512], BF16) for i in range(2)]
            bG = [Buf("C_G%d" % i) for i in range(2)]
            rG = Rot([0, 1])
            YT = sbuf(sk, "C_YT", [128, 8, 512], BF16)
            bYT = [Buf("C_YT%d" % f) for f in range(8)]
            h1T = sbuf(sk, "C_h1T", [128, 8, 512], BF16)
            bh1T = [Buf("C_h1T%d" % t) for t in range(4)]
            bh1T2 = [Buf("C_h1Tb%d" % t) for t in range(4)]
            actT = sbuf(sk, "C_actT", [128, 22, 512], BF16)
            bactT = [Buf("C_actT%d" % f) for f in range(22)]
            acc = sbuf(sk, "C_acc", [128, 4, DM], F32)
            bacc = [Buf("C_acc%d" % t) for t in range(4)]
            tA = [sbuf(sk, "C_tA%d" % i, [128, 512], F32) for i in range(2)]
            btA = [Buf("C_tA%d" % i) for i in range(2)]
            rA = Rot([0, 1])
            stt = sbuf(sk, "C_stt", [128, 8, 2, 6], F32)
            mv = sbuf(sk, "C_mv", [128, 8, 4], F32)
            bst = [Buf("C_st%d" % j) for j in range(8)]
            epst = sbuf(sk, "C_eps", [128, 1], F32); beps = Buf("C_eps")
            outs = [sbuf(sk, "C_out%d" % i, [128, DM], F32) for i in range(2)]
            bouts = [Buf("C_out%d" % i) for i in range(2)]
            rO = Rot([0, 1])
            identf = sbuf(sk, "C_identf", [128, 128], F32); bident = Buf("C_identf")
            rp = Rot(list(range(8)))

            T.dma("sp", identf[:], c_identf, [], [bident], bident)
            T.dma("sp", lncol[:], lncol_in, [], [blncol], blncol)
            T.op("dve", lambda: nc.vector.memset(epst[:], 1e-5), [], [beps])
            for r in range(4):
                T.dma("sp", lnp[:, r, :], ln_in[r:r + 1, :].to_broadcast([128, DM]), [], [blnp], blnp)

            w_gate = []
            for fh in range(2):
                w_gate += [(wgmb, 0, 8, fh * 512, 512), (wgmb, 0, 8, 1024 + fh * 512, 512), (wpab, 0, 4, fh * 512, 512), (wpbb, 0, 4, fh * 512, 512)]
            w_mix = [(woutb, 0, 8, n * 512, 512) for n in range(2)]
            w_rest = []
            for f0 in range(0, 22, 4):
                nfl = min(4, 22 - f0)
                w_rest += [(wgb, 0, 8, f0 * 128, nfl * 128), (wub, 0, 8, f0 * 128, nfl * 128)]
            for n in range(2):
                w_rest += [(wdb, k0, nk, n * 512, 512) for (k0, nk) in [(0, 8), (8, 8), (16, 6)]]
            w_rest += [(wpgb, 0, 8, n * 512, 512) for n in range(2)] + [(wpleb, 0, 2, n * 512, 512) for n in range(2)]
            n_mt = 8 if mts is None else len(mts)
            wseq = list(w_gate)
            for mi_ in range(n_mt):
                wseq += w_mix + (w_gate if mi_ + 1 < n_mt else []) + w_rest
            wstate = {"issued": 0, "used": 0}
            PF = 4

            def wload(src2d, k0, nk, c0, ncol):
                u = wstate["used"]
                assert wseq[u] == (src2d, k0, nk, c0, ncol), (u, wseq[u][1:], (k0, nk, c0, ncol))
                while wstate["issued"] < min(len(wseq), u + PF + 1):
                    j = wstate["issued"]
                    (sr, a0, an, b0, bn) = wseq[j]
                    v = sr.rearrange("(k p) n -> p k n", p=128)
                    T.dma("sp", WP[j % NW][:, 0:an, 0:bn], v[:, a0:a0 + an, b0:b0 + bn], [B_prepC], [bWP[j % NW]], bWP[j % NW])
                    wstate["issued"] += 1
                wstate["used"] += 1
                return u % NW

            def ln_stats(src_ap, bsrc, j):
                for hh in range(2):
                    T.op("dve", lambda: nc.vector.bn_stats(out=stt[:, j, hh, :], in_=src_ap[:, hh * 512:(hh + 1) * 512]), [bsrc], [bst[j]])
                T.op("dve", lambda: nc.vector.bn_aggr(out=mv[:, j, 0:2], in_=stt[:, j, :, :].rearrange("p a b -> p (a b)")), [bst[j]], [bst[j]])
                T.op("act", lambda: nc.scalar.activation(out=mv[:, j, 2:3], in_=mv[:, j, 1:2], func=AF.Sqrt, bias=epst[:, 0:1]),
                     [bst[j], beps], [bst[j]])
                T.op("dve", lambda: nc.vector.reciprocal(out=mv[:, j, 2:3], in_=mv[:, j, 2:3]), [bst[j]], [bst[j]])
                T.op("dve", lambda: nc.vector.scalar_tensor_tensor(out=mv[:, j, 3:4], in0=mv[:, j, 0:1], scalar=-1.0, in1=mv[:, j, 2:3],
                                                                  op0=ALU.mult, op1=ALU.mult), [bst[j]], [bst[j]])
                T.op("act", lambda: nc.scalar.activation(out=src_ap, in_=src_ap, func=AF.Identity, scale=mv[:, j, 2:3], bias=mv[:, j, 3:4]),
                     [bsrc, bst[j]], [bsrc])

            mlist = list(range(8)) if mts is None else mts
            xv = xTb[s].rearrange("(k p) n -> p k n", p=128)
            pv_ = pTb[s].rearrange("(k p) n -> p k n", p=128)
            def load_gate_inputs(mt_):
                tk = slice(mt_ * 512, (mt_ + 1) * 512)
                for k in range(8):
                    T.dma("sp", xTs[:, k, :], xv[:, k, tk], [B_prepX[s]], [bxT], bxT)
                T.dma("sp", oTa[:], OT_A[:, :, tk], [D["OT_A"]], [boTa], boTa)
                T.dma("sp", oTb[:], OT_B[:, :, tk], [D["OT_B"]], [boTb], boTb)

            load_gate_inputs(mlist[0])
            def gate_step(fhs=(0, 1)):
                for fh in fhs:
                    wg0 = wload(wgmb, 0, 8, fh * 512, 512)
                    wg1 = wload(wgmb, 0, 8, 1024 + fh * 512, 512)
                    wpa = wload(wpab, 0, 4, fh * 512, 512)
                    wpb = wload(wpbb, 0, 4, fh * 512, 512)
                    for fl in range(4):
                        f = fh * 4 + fl
                        cs = slice(fl * 128, (fl + 1) * 128)
                        gi = rG.next()
                        pg = [rp.next(), rp.next()]
                        for a, wgx in enumerate([wg0, wg1]):
                            for k in range(8):
                                T.op("pe", lambda: nc.tensor.matmul(PS[pg[a]][:], lhsT=WP[wgx][:, k, cs], rhs=xTs[:, k, :],
                                                                   start=(k == 0), stop=(k == 7)), [bWP[wgx], bxT], [PB[pg[a]]], inc=(k == 7))
                            T.op("act", lambda: nc.scalar.activation(out=G[gi][:, a, :], in_=PS[pg[a]][:], func=AF.Sigmoid),
                                 [PB[pg[a]]], [bG[gi]])
                        pa = rp.next(); pb = rp.next()
                        for (pp, wx, ox, box) in [(pa, wpa, oTa, boTa), (pb, wpb, oTb, boTb)]:
                            for k in range(4):
                                T.op("pe", lambda: nc.tensor.matmul(PS[pp][:], lhsT=WP[wx][:, k, cs], rhs=ox[:, k, :],
                                                                   start=(k == 0), stop=(k == 3)), [bWP[wx], box], [PB[pp]], inc=(k == 3))
                        a_ = rA.next()
                        T.op("dve", lambda: nc.vector.tensor_tensor(out=tA[a_][:], in0=PS[pa][:], in1=G[gi][:, 0, :], op=ALU.mult),
                             [PB[pa], bG[gi]], [btA[a_]])
                        b_ = rA.next()
                        T.op("dve", lambda: nc.vector.tensor_tensor(out=tA[b_][:], in0=PS[pb][:], in1=G[gi][:, 1, :], op=ALU.mult),
                             [PB[pb], bG[gi]], [btA[b_]])
                        T.op("dve", lambda: nc.vector.tensor_tensor(out=YT[:, f, :], in0=tA[a_][:], in1=tA[b_][:], op=ALU.add),
                             [btA[a_], btA[b_]], [bYT[f]])

            def mix_ln1(mt):
                tok = slice(mt * 512, (mt + 1) * 512)
                T.dma("sp", pTs[:], pv_[:, :, tok], [B_prepP[s]], [bpT], bpT)
                for t in range(4):
                    r0 = mt * 512 + t * 128
                    T.dma("sp", xtm[:, t, :], x_in[s, r0:r0 + 128, :], [], [bxtm[t]], bxtm[t])
                wo = [wload(woutb, 0, 8, n * 512, 512) for n in range(2)]
                for t in range(4):
                    for n in range(2):
                        pb = rp.next()
                        for k in range(8):
                            T.op("pe", lambda: nc.tensor.matmul(PS[pb][:], lhsT=YT[:, k, t * 128:(t + 1) * 128], rhs=WP[wo[n]][:, k, :],
                                                               start=(k == 0), stop=(k == 7)), [bYT[k], bWP[wo[n]]], [PB[pb]], inc=(k == 7))
                        T.op("dve", lambda: nc.vector.scalar_tensor_tensor(out=xtm[:, t, n * 512:(n + 1) * 512], in0=xtm[:, t, n * 512:(n + 1) * 512],
                                                                          scalar=ALPHA, in1=PS[pb][:], op0=ALU.mult, op1=ALU.add),
                             [bxtm[t], PB[pb]], [bxtm[t]])
                    ln_stats(xtm[:, t, :], bxtm[t], t)
            def ln1_tr(mt):
                for t in range(4):
                    for hh in range(2):
                        pb = rp.next()
                        for j in range(4):
                            k = hh * 4 + j
                            T.op("pe", lambda: nc.tensor.transpose(PS[pb][:, j * 128:(j + 1) * 128], xtm[:, t, k * 128:(k + 1) * 128], identf[:]),
                                 [bxtm[t], bident], [PB[pb]], inc=(j == 3))
                        for j in range(4):
                            k = hh * 4 + j
                            if hh == 0:
                                T.op("act", lambda: nc.scalar.activation(out=h1T[:, k, t * 128:(t + 1) * 128], in_=PS[pb][:, j * 128:(j + 1) * 128],
                                                                        func=AF.Identity, scale=lncol[:, 0, k:k + 1], bias=lncol[:, 1, k:k + 1]),
                                     [PB[pb], blncol], [bh1T[t]])
                            else:
                                T.op("dve", lambda: nc.vector.tensor_scalar(out=h1T[:, k, t * 128:(t + 1) * 128], in0=PS[pb][:, j * 128:(j + 1) * 128],
                                                                           scalar1=lncol[:, 0, k:k + 1], scalar2=lncol[:, 1, k:k + 1],
                                                                           op0=ALU.mult, op1=ALU.add), [PB[pb], blncol], [bh1T2[t]])
                    T.op("pool", lambda: nc.gpsimd.tensor_tensor(out=xtm[:, t, :], in0=xtm[:, t, :], in1=lnp[:, 0, :], op=ALU.mult),
                         [bxtm[t], blnp], [bxtm[t]])
                    T.op("pool", lambda: nc.gpsimd.tensor_tensor(out=xtm[:, t, :], in0=xtm[:, t, :], in1=lnp[:, 1, :], op=ALU.add),
                         [bxtm[t], blnp], [bxtm[t]])

            def rest(mt):
                for f0 in range(0, 22, 4):
                    nfl = min(4, 22 - f0)
                    wg_ = wload(wgb, 0, 8, f0 * 128, nfl * 128)
                    wu_ = wload(wub, 0, 8, f0 * 128, nfl * 128)
                    for fl in range(nfl):
                        cs = slice(fl * 128, (fl + 1) * 128)
                        pg = rp.next(); pu = rp.next()
                        for (pp, wx) in [(pg, wg_), (pu, wu_)]:
                            for k in range(8):
                                T.op("pe", lambda: nc.tensor.matmul(PS[pp][:], lhsT=WP[wx][:, k, cs], rhs=h1T[:, k, :],
                                                                   start=(k == 0), stop=(k == 7)), [bWP[wx]] + bh1T + bh1T2, [PB[pp]], inc=(k == 7))
                        a_ = rA.next()
                        T.op("act", lambda: nc.scalar.activation(out=tA[a_][:], in_=PS[pg][:], func=AF.Silu), [PB[pg]], [btA[a_]])
                        T.op("dve", lambda: nc.vector.tensor_tensor(out=actT[:, f0 + fl, :], in0=tA[a_][:], in1=PS[pu][:], op=ALU.mult),
                             [btA[a_], PB[pu]], [bactT[f0 + fl]])
                for n in range(2):
                    wd_ = [wload(wdb, k0, nk, n * 512, 512) for (k0, nk) in [(0, 8), (8, 8), (16, 6)]]
                    for t in range(4):
                        pb = rp.next()
                        for k in range(22):
                            T.op("pe", lambda: nc.tensor.matmul(PS[pb][:], lhsT=actT[:, k, t * 128:(t + 1) * 128], rhs=WP[wd_[k // 8]][:, k % 8, :],
                                                               start=(k == 0), stop=(k == 21)), [bactT[k], bWP[wd_[k // 8]]], [PB[pb]], inc=(k == 21))
                        T.op("dve", lambda: nc.vector.scalar_tensor_tensor(out=acc[:, t, n * 512:(n + 1) * 512], in0=xtm[:, t, n * 512:(n + 1) * 512],
                                                                          scalar=ALPHA, in1=PS[pb][:], op0=ALU.mult, op1=ALU.add),
                             [bxtm[t], PB[pb]], [bacc[t]])
                wpg_ = [wload(wpgb, 0, 8, n * 512, 512) for n in range(2)]
                wpl_ = [wload(wpleb, 0, 2, n * 512, 512) for n in range(2)]
                for t in range(4):
                    for n in range(2):
                        pg = rp.next(); pl = rp.next()
                        for k in range(8):
                            T.op("pe", lambda: nc.tensor.matmul(PS[pg][:], lhsT=h1T[:, k, t * 128:(t + 1) * 128], rhs=WP[wpg_[n]][:, k, :],
                                                               start=(k == 0), stop=(k == 7)), [bh1T[t], bh1T2[t], bWP[wpg_[n]]], [PB[pg]], inc=(k == 7))
                        for k in range(2):
                            T.op("pe", lambda: nc.tensor.matmul(PS[pl][:], lhsT=pTs[:, k, t * 128:(t + 1) * 128], rhs=WP[wpl_[n]][:, k, :],
                                                               start=(k == 0), stop=(k == 1)), [bpT, bWP[wpl_[n]]], [PB[pl]], inc=(k == 1))
                        a_ = rA.next()
                        T.op("act", lambda: nc.scalar.activation(out=tA[a_][:], in_=PS[pg][:], func=AF.Sigmoid), [PB[pg]], [btA[a_]])
                        T.op("dve", lambda: nc.vector.tensor_tensor(out=tA[a_][:], in0=tA[a_][:], in1=PS[pl][:], op=ALU.mult),
                             [btA[a_], PB[pl]], [btA[a_]])
                        dsl = acc[:, t, n * 512:(n + 1) * 512]
                        T.op("dve", lambda: nc.vector.tensor_tensor(out=dsl, in0=dsl, in1=tA[a_][:], op=ALU.add), [bacc[t], btA[a_]], [bacc[t]])
                    ln_stats(acc[:, t, :], bacc[t], 4 + t)
                    o = rO.next()
                    T.op("pool", lambda: nc.gpsimd.tensor_tensor(out=outs[o][:], in0=acc[:, t, :], in1=lnp[:, 2, :], op=ALU.mult),
                         [bacc[t], blnp], [bouts[o]])
                    T.op("pool", lambda: nc.gpsimd.tensor_tensor(out=outs[o][:], in0=outs[o][:], in1=lnp[:, 3, :], op=ALU.add),
                         [bouts[o], blnp], [bouts[o]])
                    r0 = mt * 512 + t * 128
                    T.dma("pool", out_d[s, r0:r0 + 128, :], outs[o][:], [bouts[o]], [D["out"]], bouts[o])

            gate_step()
            if len(mlist) > 1:
                load_gate_inputs(mlist[1])
            for mi, mt in enumerate(mlist):
                nxt = mi + 1 < len(mlist)
                mix_ln1(mt)
                if nxt:
                    gate_step((0,))
                ln1_tr(mt)
                if nxt:
                    gate_step((1,))
                    if mi + 2 < len(mlist):
                        load_gate_inputs(mlist[mi + 2])
                rest(mt)
        T.barrier()

    prep_small()
    prep_big()
    if "A" in phases:
        phase_a(list(range(nseq)))
    for s in range(nseq):
        if "B" in phases:
            phase_b(s)
        prep_emit(len(prep_items))
        if "C" in phases:
            phase_c(s)
    T.barrier()
    T._wait("sp", [(k, v) for k, v in T.cnt.items() if v > 0])
    build_nc.stats = (T.n_inst, T.n_wait, len(T.sems))
    return nc


def _prep_inputs(inputs):
    f = lambda a: np.ascontiguousarray(np.asarray(a))
    x = f(inputs["x"]); p = f(inputs["p"])[0]; pos = f(inputs["positions"])
    w_in = f(inputs["w_in"])[0]
    cols = _wa_columns()
    shared = {
        "wa": np.ascontiguousarray(w_in[:, cols]),
        "wgm": np.ascontiguousarray(w_in[:, 1944:3992]),
        "sinks": f(inputs["attn_sinks"]).reshape(1, 8),
        "wpa": f(inputs["w_proj_swa"])[0], "wpb": f(inputs["w_proj_nsa"])[0], "wout": f(inputs["w_out"])[0],
        "wg": f(inputs["w_ff_gate"])[0], "wu": f(inputs["w_ff_up"])[0], "wd": f(inputs["w_ff_down"])[0],
        "wple": f(inputs["w_ple"])[0], "wpg": f(inputs["w_ple_gate"])[0],
        "ln": np.ascontiguousarray(np.stack([f(inputs["ln1_g"])[0], f(inputs["ln1_b"])[0], f(inputs["ln2_g"])[0], f(inputs["ln2_b"])[0]])),
        "w2k": f(inputs["w_cmp_k2"])[0], "w2v": f(inputs["w_cmp_v2"])[0],
    }
    shared["lncol"] = np.ascontiguousarray(np.stack([f(inputs["ln1_g"])[0].reshape(8, 128).T, f(inputs["ln1_b"])[0].reshape(8, 128).T], axis=1))
    pe = f(inputs["cmp_pos_emb"])[0]
    peT = np.transpose(pe, (2, 0, 1))
    shared["peT"] = np.ascontiguousarray(np.concatenate([peT[:, :, :16], peT[:, :, 16:]], axis=0))
    for nm, key in [("w1k", "w_cmp_k1"), ("w1v", "w_cmp_v1")]:
        w1 = f(inputs[key])[0].reshape(32, 64, 256).transpose(1, 0, 2)
        shared[nm] = np.ascontiguousarray(np.concatenate([w1[:, :16], w1[:, 16:]], axis=0))
    for k, v in _consts().items():
        shared["c_" + k] = v
    in_maps = []
    for c in range(8):
        sl = slice(c * NSEQ, (c + 1) * NSEQ)
        m = dict(shared)
        m["x"] = np.ascontiguousarray(x[sl])
        m["xT"] = np.ascontiguousarray(np.transpose(x[sl], (0, 2, 1)))
        m["pT"] = np.ascontiguousarray(np.transpose(p[sl], (0, 2, 1)))
        m["pos"] = np.ascontiguousarray(pos[sl].astype(np.int32))
        in_maps.append(m)
    return in_maps


def kernel(**inputs):
    in_maps = _prep_inputs(inputs)
    nc = build_nc()
    res = run_bass_kernel_spmd(nc, in_maps, core_ids=list(range(8)))
    out = np.concatenate([np.asarray(r["out"]) for r in res.results], axis=0)
    return out.astype(np.float32)
```
